# Optimizing a Trainium2 kernel written in Bass

```python
import math
import jax, jax.numpy as jnp
from jax import lax
import numpy as np

D_MODEL = 1024
BATCH = 8
SEQ = 2048
DEPTH = 1

N_MEM = 256
CONV_DIM = 512
CONV_WIDTH = 31
N_HEADS = 8
N_KV_HEADS = 2
HEAD_DIM = 64
WINDOW = 128
BLOCK = 128
N_MEM_HEADS = 4
MEM_HEAD_DIM = 128
N_BUCKETS = 32
MAX_DISTANCE = 128
N_BRANCHES = 3
D_FF = int(math.ceil(8 * D_MODEL / 3 / 256)) * 256
DEEPNORM_ALPHA = (2 * DEPTH) ** 0.25
DEEPNORM_BETA = (8 * DEPTH) ** -0.25
LN_EPS = 1e-5
NEG_INF = -1e30

ATTN_Q_DIM = N_HEADS * HEAD_DIM
KV_DIM = N_KV_HEADS * HEAD_DIM
MEM_DIM = N_MEM_HEADS * MEM_HEAD_DIM
IN_WIDTHS = (2 * CONV_DIM, ATTN_Q_DIM, KV_DIM, KV_DIM, MEM_DIM, N_BRANCHES * D_MODEL)
IN_DIM = sum(IN_WIDTHS)

kernel_name = "hybrid_conv_window_gqa_memory_encoder"


def split_points():
    pts, acc = [], 0
    for w in IN_WIDTHS[:-1]:
        acc += w
        pts.append(acc)
    return pts


def layer_norm(x, g, b):
    xf = x.astype(jnp.float32)
    mu = jnp.mean(xf, axis=-1, keepdims=True)
    var = jnp.mean(jnp.square(xf - mu), axis=-1, keepdims=True)
    return ((xf - mu) * lax.rsqrt(var + LN_EPS)).astype(x.dtype) * g + b


def t5_bucket(rel):
    half = N_BUCKETS // 2
    max_exact = half // 2
    base = jnp.where(rel > 0, half, 0)
    n = jnp.abs(rel)
    nf = jnp.maximum(n, 1).astype(jnp.float32)
    large = max_exact + (jnp.log(nf / max_exact) / math.log(MAX_DISTANCE / max_exact)
                         * (half - max_exact)).astype(jnp.int32)
    large = jnp.minimum(large, half - 1)
    return base + jnp.where(n < max_exact, n, large)


def conformer_conv(glu_in, dw_w, dw_b, ln_g, ln_b, w_out):
    a, g = jnp.split(glu_in, 2, axis=-1)
    u = a * jax.nn.sigmoid(g)
    pad = CONV_WIDTH // 2
    u = lax.conv_general_dilated(u, dw_w[:, None, :], window_strides=(1,),
                                 padding=[(pad, pad)],
                                 dimension_numbers=('NWC', 'WIO', 'NWC'),
                                 feature_group_count=CONV_DIM) + dw_b
    u = jax.nn.silu(layer_norm(u, ln_g, ln_b))
    return u @ w_out


def windowed_gqa(q, k, v, rel_bias, sink):
    B, S = q.shape[0], q.shape[1]
    nb = S // BLOCK
    G = N_HEADS // N_KV_HEADS
    qb = q.reshape(B, nb, BLOCK, N_KV_HEADS, G, HEAD_DIM)
    pad = ((0, 0), (BLOCK, BLOCK), (0, 0), (0, 0))
    kr = jnp.pad(k, pad).reshape(B, nb + 2, BLOCK, N_KV_HEADS, HEAD_DIM)
    vr = jnp.pad(v, pad).reshape(B, nb + 2, BLOCK, N_KV_HEADS, HEAD_DIM)
    kb = jnp.concatenate([kr[:, :-2], kr[:, 1:-1], kr[:, 2:]], axis=2)
    vb = jnp.concatenate([vr[:, :-2], vr[:, 1:-1], vr[:, 2:]], axis=2)

    qloc = jnp.arange(BLOCK, dtype=jnp.int32)
    kloc = jnp.arange(3 * BLOCK, dtype=jnp.int32) - BLOCK
    rel = kloc[None, :] - qloc[:, None]
    bias = rel_bias[t5_bucket(rel)].astype(jnp.float32)
    bias = jnp.transpose(bias, (2, 0, 1)).reshape(N_KV_HEADS, G, BLOCK, 3 * BLOCK)
    kpos = jnp.arange(nb, dtype=jnp.int32)[:, None, None] * BLOCK + kloc[None, None, :]
    valid = (jnp.abs(rel) <= WINDOW)[None] & (kpos >= 0) & (kpos < S)

    scale = HEAD_DIM ** -0.5
    s = jnp.einsum('bnqhgd,bnkhd->bnhgqk', qb, kb,
                   preferred_element_type=jnp.float32) * scale + bias
    s = jnp.where(valid[None, :, None, None], s, NEG_INF)
    sink_col = jnp.broadcast_to(sink.astype(jnp.float32).reshape(N_KV_HEADS, G, 1, 1),
                                s.shape[:-1] + (1,))
    p = jax.nn.softmax(jnp.concatenate([s, sink_col], axis=-1), axis=-1)[..., :-1]
    o = jnp.einsum('bnhgqk,bnkhd->bnqhgd', p.astype(v.dtype), vb)
    return o.reshape(B, S, N_HEADS * HEAD_DIM)


def memory_attention(q_mem, mem, w_mem_kv):
    B, S = q_mem.shape[0], q_mem.shape[1]
    q = q_mem.reshape(B, S, N_MEM_HEADS, MEM_HEAD_DIM)
    km, vm = jnp.split(mem @ w_mem_kv, 2, axis=-1)
    km = km.reshape(B, N_MEM, N_MEM_HEADS, MEM_HEAD_DIM)
    vm = vm.reshape(B, N_MEM, N_MEM_HEADS, MEM_HEAD_DIM)
    s = jnp.einsum('bshd,bmhd->bhsm', q, km,
                   preferred_element_type=jnp.float32) * (MEM_HEAD_DIM ** -0.5)
    p = jax.nn.softmax(s, axis=-1)
    o = jnp.einsum('bhsm,bmhd->bshd', p.astype(vm.dtype), vm)
    return o.reshape(B, S, MEM_DIM)


def hybrid_layer(x, mem, rel_bias, w_in, b_gate, conv_dw_w, conv_dw_b, conv_ln_g, conv_ln_b,
                 w_conv_out, attn_sink, w_attn_out, w_mem_kv, w_mem_out, w_o,
                 ln1_g, ln1_b, w_ffn_in, w_ffn_out, ln2_g, ln2_b):
    B, S, D = x.shape
    proj = x @ w_in
    glu_in, q, k, v, q_mem, gate_logits = jnp.split(proj, split_points(), axis=-1)

    y_conv = conformer_conv(glu_in, conv_dw_w, conv_dw_b, conv_ln_g, conv_ln_b, w_conv_out)
    y_attn = windowed_gqa(q.reshape(B, S, N_HEADS, HEAD_DIM),
                          k.reshape(B, S, N_KV_HEADS, HEAD_DIM),
                          v.reshape(B, S, N_KV_HEADS, HEAD_DIM),
                          rel_bias, attn_sink) @ w_attn_out
    y_mem = memory_attention(q_mem, mem, w_mem_kv) @ w_mem_out

    gates = jax.nn.sigmoid(gate_logits + b_gate).reshape(B, S, N_BRANCHES, D)
    merged = gates[:, :, 0] * y_conv + gates[:, :, 1] * y_attn + gates[:, :, 2] * y_mem
    x = layer_norm(DEEPNORM_ALPHA * x + merged @ w_o, ln1_g, ln1_b)

    gate, up = jnp.split(x @ w_ffn_in, 2, axis=-1)
    ffn = (jax.nn.silu(gate) * up) @ w_ffn_out
    return layer_norm(DEEPNORM_ALPHA * x + ffn, ln2_g, ln2_b)


def setup_inputs(seed: int = 0) -> dict:
    key = jax.random.key(seed)
    ks = jax.random.split(key, 24)
    f32 = jnp.float32
    L = DEPTH

    def nrm(k, shape, scale):
        return jax.random.normal(k, shape, f32) * scale

    beta = DEEPNORM_BETA
    return {
        "x": nrm(ks[0], (BATCH, SEQ, D_MODEL), 1.0),
        "mem": nrm(ks[1], (BATCH, N_MEM, D_MODEL), 1.0),
        "rel_bias": nrm(ks[2], (N_BUCKETS, N_HEADS), 0.5),
        "w_in": nrm(ks[3], (L, D_MODEL, IN_DIM), D_MODEL ** -0.5),
        "b_gate": nrm(ks[4], (L, N_BRANCHES * D_MODEL), 0.1),
        "conv_dw_w": nrm(ks[5], (L, CONV_WIDTH, CONV_DIM), CONV_WIDTH ** -0.5),
        "conv_dw_b": nrm(ks[6], (L, CONV_DIM), 0.02),
        "conv_ln_g": 1.0 + nrm(ks[7], (L, CONV_DIM), 0.02),
        "conv_ln_b": nrm(ks[8], (L, CONV_DIM), 0.02),
        "w_conv_out": nrm(ks[9], (L, CONV_DIM, D_MODEL), beta * CONV_DIM ** -0.5),
        "attn_sink": nrm(ks[10], (L, N_HEADS), 0.5),
        "w_attn_out": nrm(ks[11], (L, ATTN_Q_DIM, D_MODEL), beta * ATTN_Q_DIM ** -0.5),
        "w_mem_kv": nrm(ks[12], (L, D_MODEL, 2 * MEM_DIM), D_MODEL ** -0.5),
        "w_mem_out": nrm(ks[13], (L, MEM_DIM, D_MODEL), beta * MEM_DIM ** -0.5),
        "w_o": nrm(ks[14], (L, D_MODEL, D_MODEL), beta * D_MODEL ** -0.5),
        "ln1_g": 1.0 + nrm(ks[15], (L, D_MODEL), 0.02),
        "ln1_b": nrm(ks[16], (L, D_MODEL), 0.02),
        "w_ffn_in": nrm(ks[17], (L, D_MODEL, 2 * D_FF), D_MODEL ** -0.5),
        "w_ffn_out": nrm(ks[18], (L, D_FF, D_MODEL), beta * D_FF ** -0.5),
        "ln2_g": 1.0 + nrm(ks[19], (L, D_MODEL), 0.02),
        "ln2_b": nrm(ks[20], (L, D_MODEL), 0.02),
    }


def reference(x, mem, rel_bias, w_in, b_gate, conv_dw_w, conv_dw_b, conv_ln_g, conv_ln_b,
              w_conv_out, attn_sink, w_attn_out, w_mem_kv, w_mem_out, w_o,
              ln1_g, ln1_b, w_ffn_in, w_ffn_out, ln2_g, ln2_b):
    for l in range(DEPTH):
        x = hybrid_layer(x, mem, rel_bias, w_in[l], b_gate[l], conv_dw_w[l], conv_dw_b[l],
                         conv_ln_g[l], conv_ln_b[l], w_conv_out[l], attn_sink[l],
                         w_attn_out[l], w_mem_kv[l], w_mem_out[l], w_o[l],
                         ln1_g[l], ln1_b[l], w_ffn_in[l], w_ffn_out[l], ln2_g[l], ln2_b[l])
    return x
```

```python
import math
import numpy as np
import concourse.bass as bass
import concourse.mybir as mybir
from concourse.bass_utils import run_bass_kernel_spmd

F32 = mybir.dt.float32
BF16 = mybir.dt.bfloat16
ALU = mybir.AluOpType
AF = mybir.ActivationFunctionType

D = 1024
T = 2048
NMEM = 256
IN_DIM = 5376
DFF = 2816
ALPHA = 2.0 ** 0.25
LN_EPS = 1e-5
PERM = [0, 2, 1, 3, 4, 6, 5, 7]
KB = 1024

R1_B, R2_B, R3_B, R4_B, R5_B = 32 * KB, 64 * KB, 48 * KB, 32 * KB, 24 * KB


def _flat(deps):
    out = []
    for d in deps:
        if d is None:
            continue
        if isinstance(d, list):
            out.extend(_flat(d))
        elif isinstance(d, tuple) and len(d) > 0 and (d[0] == "E" or not isinstance(d[0], (tuple, list))):
            out.append(d)
        else:
            out.extend(_flat(d))
    return out


class Eng:
    def __init__(self, nc, eng, name, plan):
        self.nc, self.eng, self.name = nc, eng, name
        self.sem = nc.alloc_semaphore("sem_" + name)
        self.v = 0
        self.real = 0
        self.vmap = {}
        self.seen = {}
        self.plan = plan
        self.needed = set()

    def wait(self, deps):
        for tok in _flat(deps):
            if tok[0] == "E":
                _, src, v = tok
                k = id(src)
                if self.seen.get(k, 0) >= v:
                    continue
                self.seen[k] = v
                src.needed.add(v)
                if self.plan is not None:
                    self.eng.wait_ge(src.sem, src.vmap[v])
            else:
                sem, val = tok
                k = id(sem)
                if self.seen.get(k, 0) >= val:
                    continue
                self.eng.wait_ge(sem, val)
                self.seen[k] = val

    def run(self, fn, deps=(), inc=True):
        self.wait(deps)
        ins = fn()
        if inc:
            self.v += 1
            if self.plan is None or self.v in self.plan[self.name]:
                ins.then_inc(self.sem, 1)
                self.real += 1
                self.vmap[self.v] = self.real
            return ("E", self, self.v)
        return None

    def tok(self):
        return ("E", self, self.v)


class Slot:
    def __init__(self, nc, name):
        self.sem = nc.alloc_semaphore("dq_" + name)
        self.count = 0

    def tok(self):
        return (self.sem, self.count)


def build_nc(stop_after=None, taps=None):
    plan = _build(stop_after, taps, None)[1]
    return _build(stop_after, taps, plan)[0]


def _build(stop_after, taps, plan):
    nc = bass.Bass("TRN2", target_bir_lowering=False)
    taps = taps or {}

    def din(name, shape):
        return nc.dram_tensor(name, list(shape), F32, kind="ExternalInput").ap()

    x_d = din("x", [T, D])
    mem_d = din("mem", [NMEM, D])
    rbp_d = din("rbp", [32, 8])
    sinkp_d = din("sinkp", [1, 8])
    onehot_d = din("onehot", [32, 512])
    relvalid_d = din("relvalid", [1, 512])
    w_in_d = din("w_in", [D, IN_DIM])
    bgate_d = din("bgate", [128, 24])
    dwsc_d = din("dwsc", [128, 128])
    cvec_d = din("cvec", [128, 12])
    w_co_d = din("w_co", [512, D])
    w_ao_d = din("w_ao", [512, D])
    w_mkv_d = din("w_mkv", [D, D])
    w_mo_d = din("w_mo", [512, D])
    w_o_d = din("w_o", [D, D])
    ln1_d = din("ln1", [2, D])
    w_f1_d = din("w_f1", [D, 2 * DFF])
    w_f2_d = din("w_f2", [DFF, D])
    ln2_d = din("ln2", [2, D])
    out_d = nc.dram_tensor("out", [T, D], F32, kind="ExternalOutput").ap()
    gexp_d = nc.dram_tensor("gexp_scratch", [2, 8, 512], F32, kind="Internal")
    w1b_d = nc.dram_tensor("w1_bf16_cache", [D, 2 * DFF], BF16, kind="Internal").ap()

    pe = Eng(nc, nc.tensor, "pe", plan)
    act = Eng(nc, nc.scalar, "act", plan)
    dve = Eng(nc, nc.vector, "dve", plan)
    pool = Eng(nc, nc.gpsimd, "pool", plan)
    sp = Eng(nc, nc.sync, "sp", plan)
    engines = [pe, act, dve, pool, sp]

    def result():
        return nc, {e.name: set(e.needed) for e in engines}

    def barrier():
        toks = [e.tok() for e in engines]
        for e in engines:
            e.wait([t for t, f in zip(toks, engines) if f is not e])

    _slot_n = [0]

    def new_slot(name="s"):
        _slot_n[0] += 1
        return Slot(nc, f"{name}{_slot_n[0]}")

    def dma(e, out, in_, slot, deps=()):
        e.wait(deps)
        e.eng.dma_start(out=out, in_=in_).then_inc(slot.sem, 16)
        slot.count += 16
        return slot.tok()

    R1 = nc.alloc_sbuf_tensor("R1", [128, R1_B // 2], BF16)
    R2 = nc.alloc_sbuf_tensor("R2", [128, R2_B // 2], BF16)
    R3 = nc.alloc_sbuf_tensor("R3", [128, R3_B // 2], BF16)
    R4 = nc.alloc_sbuf_tensor("R4", [128, R4_B // 2], BF16)
    R5 = nc.alloc_sbuf_tensor("R5", [128, R5_B // 2], BF16)

    def V(reg, off, shape, dt, p0=0):
        esz = 2 if dt == BF16 else 4
        n = 1
        for s in shape[1:]:
            n *= s
        a = reg[p0:p0 + shape[0], off // 2: off // 2 + (n * esz) // 2]
        if dt != BF16:
            a = a.bitcast(dt)
        if len(shape) >= 3:
            names = "abcdefg"[:len(shape) - 1]
            pat = "p (" + " ".join(names) + ") -> p " + " ".join(names)
            a = a.rearrange(pat, **{nm: shape[i + 1] for i, nm in enumerate(names[:-1])})
        return a

    ident_bf = nc.alloc_sbuf_tensor("ident_bf", [128, 128], BF16)
    ident_f = nc.alloc_sbuf_tensor("ident_f", [128, 128], F32)
    ones_bf = nc.alloc_sbuf_tensor("ones_bf", [128, 128], BF16)
    ones_f = nc.alloc_sbuf_tensor("ones_f", [128, 128], F32)
    bgate = nc.alloc_sbuf_tensor("bgate_sb", [128, 24], F32)
    dwsc = nc.alloc_sbuf_tensor("dwsc_sb", [128, 128], F32)
    ident2 = nc.alloc_sbuf_tensor("ident2", [128, 64], BF16)
    cvec = nc.alloc_sbuf_tensor("cvec_sb", [128, 12], F32)
    esink8 = nc.alloc_sbuf_tensor("esink8", [128, 8], F32)
    rb33 = nc.alloc_sbuf_tensor("rb33", [32, 8], F32)
    stt = nc.alloc_sbuf_tensor("stt", [128, 4, 4], F32)
    mvt = nc.alloc_sbuf_tensor("mvt", [128, 4, 4], F32)
    x1b_extra = nc.alloc_sbuf_tensor("x1b_extra", [128, D], BF16)
    sgf_extra = nc.alloc_sbuf_tensor("sgf_extra", [128, 512], F32)
    hsum = nc.alloc_sbuf_tensor("hsum", [128, 2, 16], F32)
    eps_t = nc.alloc_sbuf_tensor("eps_t", [128, 1], F32)
    dnt = nc.alloc_sbuf_tensor("dnt", [128, 2, 8], F32)

    ps = nc.alloc_psum_tensor("ps", [128, 8, 512], F32)
    psb = ps[:].bitcast(BF16)

    class Banks:
        def __init__(self):
            self.free = [[] for _ in range(8)]
            self.busy = [False] * 8
            self.ptr = 0

        def alloc(self, n=1):
            if n == 2 and self.ptr % 2:
                self.ptr = (self.ptr + 1) % 8
            b = self.ptr
            self.ptr = (self.ptr + n) % 8
            deps = []
            for i in range(n):
                assert not self.busy[b + i], f"PSUM bank {b + i} re-allocated before its consumer was emitted"
                self.busy[b + i] = True
                deps += self.free[b + i]
                self.free[b + i] = []
            return b, deps

        def release(self, b, tok, n=1):
            for i in range(n):
                self.free[b + i].append(tok)
                self.busy[b + i] = False

    banks = Banks()

    def bank2(b):
        return ps[:, b:b + 2, :].rearrange("p a b -> p (a b)")

    def mm(out, lhsT, rhs, start, stop, deps=(), inc=False):
        return pe.run(lambda: nc.tensor.matmul(out, lhsT, rhs, start=start, stop=stop), deps, inc)

    w_in_v = w_in_d.rearrange("(kc p) n -> p kc n", p=128)

    s_small = new_slot("small")
    dma(sp, bgate[:], bgate_d, s_small)
    dma(sp, dwsc[:], dwsc_d, s_small)
    dma(sp, cvec[:], cvec_d, s_small)
    dma(sp, rb33[:], rbp_d, s_small)
    dma(sp, esink8[:], sinkp_d.partition_broadcast(128).rearrange("p a b -> p (a b)"), s_small)
    t_small = s_small.tok()

    t_ones = pool.run(lambda: nc.gpsimd.memset(ones_f[:], 1.0))
    pool.run(lambda: nc.gpsimd.memset(eps_t[:], LN_EPS))
    t_onesb = pool.run(lambda: nc.gpsimd.memset(ones_bf[:], 1.0))
    t_idz = pool.run(lambda: nc.gpsimd.memset(ident_f[:], 0.0))
    t_identf = pool.run(lambda: nc.gpsimd.affine_select(
        out=ident_f[:], in_=ident_f[:], pattern=[[-1, 128]], compare_op=ALU.not_equal,
        fill=1.0, base=0, channel_multiplier=1), [t_idz])
    t_ident = dve.run(lambda: nc.vector.tensor_copy(ident_bf[:], ident_f[:]), [t_identf])
    t_id2a = dve.run(lambda: nc.vector.tensor_copy(ident2[0:64, :], ident_f[0:64, 0:64]), [t_identf])
    t_id2 = dve.run(lambda: nc.vector.tensor_copy(ident2[64:128, :], ident_f[64:128, 64:128]), [t_identf])
    t_esink = act.run(lambda: nc.scalar.activation(esink8[:], esink8[:], AF.Exp), [t_small])

    xT = V(R1, 0, [128, 8, T], BF16)
    memT = V(R2, 48 * KB, [128, 8, NMEM], BF16)
    NXS = 6
    xs = [V(R3, i * 2 * KB, [128, D], BF16) for i in range(NXS)]
    xs_slot = [new_slot("xs") for _ in range(NXS)]
    xs_free = [[] for _ in range(NXS)]
    blocks = [("x", tb) for tb in range(16)] + [("m", mb) for mb in range(2)]
    cw = V(R3, 16 * KB, [128, 128, 64], BF16)
    t_dg_last = {}
    wa = V(R5, 0, [128, 8, 512], BF16)
    wg = V(R5, 8 * KB, [128, 8, 512], BF16)
    s_wa, s_wg = [new_slot("wa") for _ in range(4)], [new_slot("wg") for _ in range(4)]
    UW = T + 32
    U2E = V(R4, 0, [128, 4, UW], BF16)
    U2O = V(R2, 16 * KB, [128, 4, UW], BF16)
    t_u_tok = {}
    sig = [V(R5, 16 * KB + i * 4 * KB, [128, 1024], F32) for i in range(2)]
    sig_free = [[], []]
    t_xT_ev = {}

    def p0_block(i):
        kind, bi = blocks[i]
        r = i % NXS
        src = x_d[bi * 128:(bi + 1) * 128, :] if kind == "x" else mem_d[bi * 128:(bi + 1) * 128, :]
        t_ld = dma(pool, xs[r], src, xs_slot[r], xs_free[r])
        b, bdeps = banks.alloc(1)
        for kc in range(8):
            t_tr = pe.run(lambda kc=kc: nc.tensor.transpose(
                psb[:, b, kc * 128:(kc + 1) * 128], xs[r][:, kc * 128:(kc + 1) * 128], ident_bf[:]),
                [t_ld, t_ident, bdeps] if kc == 0 else (), inc=(kc == 7))
        xs_free[r] = [t_tr]
        dst = xT[:, :, bi * 128:(bi + 1) * 128] if kind == "x" else memT[:, :, bi * 128:(bi + 1) * 128]
        srcp = psb[:, b, :].rearrange("p (a b) -> p a b", a=8)
        if i % 2 == 0:
            t_ev = act.run(lambda: nc.scalar.copy(dst, srcp), [t_tr])
        else:
            t_ev = dve.run(lambda: nc.vector.tensor_copy(dst, srcp), [t_tr])
        banks.release(b, t_ev)
        t_xT_ev[i] = t_ev
        for cj in range(i * 8, min(128, i * 8 + 8)):
            if i % 2 == 0:
                t_dg_last["dve"] = dve.run(lambda cj=cj: nc.vector.tensor_scalar(
                    out=cw[:, cj, :], in0=ident2[:], scalar1=dwsc[:, cj:cj + 1], scalar2=None, op0=ALU.mult),
                    [t_small, t_id2a, t_id2])
            else:
                t_dg_last["act"] = act.run(lambda cj=cj: nc.scalar.activation(
                    cw[:, cj, :], ident2[:], AF.Copy, scale=dwsc[:, cj:cj + 1]), [t_small, t_id2a, t_id2])

    a1_i = [0]
    sig4 = [V(R5, 16 * KB + i * 2 * KB, [128, 512], F32) for i in range(4)]
    sig4_free = [[], [], [], []]

    def a1_unit(c, q):
        xdeps = [t_xT_ev[i] for i in range(q * 4, q * 4 + 4)]
        tsl = slice(q * 512, (q + 1) * 512)
        bA, dA = banks.alloc(1)
        bG, dG = banks.alloc(1)
        for kc in range(8):
            tA = mm(ps[:, bA, :], wa[:, kc, c * 128:(c + 1) * 128], xT[:, kc, tsl], kc == 0, kc == 7,
                    [dA, t_wa[c], xdeps] if kc == 0 else (), inc=(kc == 7))
        for kc in range(8):
            tG = mm(ps[:, bG, :], wg[:, kc, c * 128:(c + 1) * 128], xT[:, kc, tsl], kc == 0, kc == 7,
                    [dG, t_wg[c]] if kc == 0 else (), inc=(kc == 7))
        r = a1_i[0] % 4
        a1_i[0] += 1
        t_sig = act.run(lambda: nc.scalar.activation(sig4[r], ps[:, bG, :], AF.Sigmoid), [tG, sig4_free[r]])
        banks.release(bG, t_sig)
        osl = slice(15 + q * 512, 15 + (q + 1) * 512)
        t_u0 = dve.run(lambda: nc.vector.tensor_tensor(
            out=U2E[0:64, c, osl], in0=ps[0:64, bA, :], in1=sig4[r][0:64, :], op=ALU.mult), [tA, t_sig])
        t_u = dve.run(lambda: nc.vector.tensor_tensor(
            out=U2O[64:128, c, osl], in0=ps[64:128, bA, :], in1=sig4[r][64:128, :], op=ALU.mult), [tA, t_sig])
        banks.release(bA, t_u)
        sig4_free[r] = [t_u]
        t_u_tok[c, q] = t_u

    t_wa, t_wg = [None] * 4, [None] * 4

    def load_glu_w(c):
        t_wa[c] = dma(pool, wa[:, :, c * 128:(c + 1) * 128], w_in_v[:, :, c * 128:(c + 1) * 128], s_wa[c])
        t_wg[c] = dma(pool, wg[:, :, c * 128:(c + 1) * 128], w_in_v[:, :, 512 + c * 128:512 + (c + 1) * 128], s_wg[c])

    for i in range(5):
        p0_block(i)
        if i == 1:
            load_glu_w(0)
        if i == 3:
            load_glu_w(1)
    t_pads = [pool.run(lambda: nc.gpsimd.memset(U2E[0:64, :, 0:15], 0.0)),
              pool.run(lambda: nc.gpsimd.memset(U2E[0:64, :, T + 15:UW], 0.0)),
              pool.run(lambda: nc.gpsimd.memset(U2O[64:128, :, 0:15], 0.0)),
              pool.run(lambda: nc.gpsimd.memset(U2O[64:128, :, T + 15:UW], 0.0))]
    rest = list(range(5, 18))
    s_shift = [new_slot("shift") for _ in range(4)]
    t_shift = [None] * 4
    for q in range(4):
        for c in range(4):
            a1_unit(c, q)
            if rest:
                p0_block(rest.pop(0))
            if q == 0 and c < 2:
                load_glu_w(c + 2)
            if q == 3:
                dma(sp, U2E[64:128, c, 0:UW - 1], U2E[0:64, c, 1:UW], s_shift[c],
                    [[t_u_tok[c, qq] for qq in range(4)], t_pads])
                t_shift[c] = dma(sp, U2O[0:64, c, 0:UW - 1], U2O[64:128, c, 1:UW], s_shift[c])
    assert not rest
    t_dg = [t_dg_last["dve"], t_dg_last["act"]]
    if stop_after == "a1":
        barrier()
        finish(nc, locals())
        return result()

    sT = V(R3, 0, [128, 4, T], BF16)
    cvb = [V(R5, i * 2 * KB, [128, 512], F32) for i in range(4)]
    sqb = [V(R5, 8 * KB + i * 2 * KB, [128, 512], F32) for i in range(2)]
    meanb = V(R5, 12 * KB, [128, 512], F32)
    varb = V(R5, 14 * KB, [128, 512], F32)
    rstdb = V(R5, 16 * KB, [128, 512], F32)
    zb = [V(R5, 18 * KB + i * 2 * KB, [128, 512], F32) for i in range(3)] + [V(R3, 40 * KB, [128, 512], F32)]
    wq = V(R4, 16640, [128, 8, 512], BF16)
    wkv = V(R4, 16640 + 8 * KB, [128, 8, 384], BF16)
    s_wq, s_wkv = new_slot("wq"), new_slot("wkv")
    t_wq = dma(pool, wq, w_in_v[:, :, 1024:1536], s_wq)
    for g in range(2):
        for dup in range(2):
            dma(pool, wkv[:, :, g * 128 + dup * 64: g * 128 + (dup + 1) * 64],
                w_in_v[:, :, 1536 + g * 64: 1536 + (g + 1) * 64], s_wkv)
    t_wkv = dma(pool, wkv[:, :, 256:384], w_in_v[:, :, 1664:1792], s_wkv)

    s_w1c = new_slot("w1c")
    for kc in range(8):
        t_w1c = dma(pool, w1b_d[kc * 128:(kc + 1) * 128, :], w_f1_d[kc * 128:(kc + 1) * 128, :], s_w1c,
                    [t_shift] if kc == 0 else ())
    cvb2 = [cvb, [V(R3, 32 * KB + i * 2 * KB, [128, 512], F32) for i in range(4)]]
    cv_free = [[[] for _ in range(4)] for _ in range(2)]
    sq_free = [[], []]
    z_free = [[], [], [], []]
    cst = {"zi": 0, "sqi": 0, "stat_free": []}

    def cv_A(tt):
        cvs = cvb2[tt % 2]
        bS, dS = banks.alloc(1)
        bQ, dQ = banks.alloc(1)
        t_cv = [None] * 4
        pend = None
        for c in range(4):
            bC, dC = banks.alloc(1)
            for jg in range(16):
                win = slice(tt * 512 + 2 * jg, tt * 512 + 2 * jg + 512)
                mm(ps[0:64, bC, :], cw[:, c * 16 + jg, :], U2E[:, c, win],
                   jg == 0, jg == 15, [dC, t_dg, t_shift[c]] if jg == 0 else ())
                tk = mm(ps[64:128, bC, :], cw[:, 64 + c * 16 + jg, :], U2O[:, c, win],
                        jg == 0, jg == 15, (), inc=(jg == 15))
            t_cv[c] = act.run(lambda c=c: nc.scalar.activation(
                cvs[c], ps[:, bC, :], AF.Identity, bias=cvec[:, c:c + 1]), [tk, cv_free[tt % 2][c]])
            r = cst["sqi"] % 2
            cst["sqi"] += 1
            t_sq = act.run(lambda c=c: nc.scalar.activation(
                sqb[r], ps[:, bC, :], AF.Square, bias=cvec[:, c:c + 1]), [sq_free[r]])
            banks.release(bC, t_sq)

            def stats_mm(c=c, r=r, t_sq=t_sq):
                mm(ps[:, bS, :], ones_f[:], cvs[c], c == 0, c == 3, [t_cv[c], dS, t_ones] if c == 0 else [t_cv[c]])
                tq_ = mm(ps[:, bQ, :], ones_f[:], sqb[r], c == 0, c == 3, [t_sq, dQ] if c == 0 else [t_sq], inc=True)
                sq_free[r] = [tq_]
                return tq_
            if pend is not None:
                pend()
            pend = stats_mm
        tq = pend()
        return (bS, bQ, tq, t_cv)

    def cv_B(tt, st):
        bS, bQ, tq, t_cv = st
        t_mean = dve.run(lambda: nc.vector.tensor_scalar(
            out=meanb, in0=ps[:, bS, :], scalar1=1.0 / 512, scalar2=None, op0=ALU.mult), [tq, cst["stat_free"]])
        t_msq = dve.run(lambda: nc.vector.tensor_tensor(out=varb, in0=meanb, in1=meanb, op=ALU.mult), [t_mean])
        t_var = dve.run(lambda: nc.vector.scalar_tensor_tensor(
            out=varb, in0=ps[:, bQ, :], scalar=1.0 / 512, in1=varb, op0=ALU.mult, op1=ALU.subtract), [t_msq])
        t_sd = act.run(lambda: nc.scalar.activation(rstdb, varb, AF.Sqrt, bias=eps_t[:, 0:1]), [t_var])
        t_rstd = dve.run(lambda: nc.vector.reciprocal(rstdb, rstdb), [t_sd])
        banks.release(bS, t_mean)
        banks.release(bQ, t_var)
        cst["stat_free"] = []
        return (t_mean, t_rstd, t_cv)

    def cv_C(tt, st):
        t_mean, t_rstd, t_cv = st
        cvs = cvb2[tt % 2]
        for c in range(4):
            r = cst["zi"] % 4
            cst["zi"] += 1
            t_z0 = dve.run(lambda c=c: nc.vector.tensor_tensor(out=zb[r], in0=cvs[c], in1=meanb, op=ALU.subtract),
                           [t_cv[c], t_mean, z_free[r]])
            t_z1 = dve.run(lambda: nc.vector.tensor_tensor(out=zb[r], in0=zb[r], in1=rstdb, op=ALU.mult),
                           [t_z0, t_rstd])
            t_s = act.run(lambda c=c: nc.scalar.activation(
                sT[:, c, tt * 512:(tt + 1) * 512], zb[r], AF.Silu,
                bias=cvec[:, 8 + c:9 + c], scale=cvec[:, 4 + c:5 + c]), [t_z1])
            z_free[r] = [t_s]
            cv_free[tt % 2][c] = [t_z0]
            cst["stat_free"].append(t_z1)

    stB = None
    for tt in range(5):
        stA = cv_A(tt) if tt < 4 else None
        if tt >= 1:
            cv_C(tt - 1, stB)
        if tt < 4:
            stB = cv_B(tt, stA)
    qT = V(R2, 0, [128, 4, T], BF16)
    evi = 0

    def proj_fm(wsl, col0, tw, dst):
        nonlocal evi
        for th in range(2):
            bb, dd = banks.alloc(2)
            for half in range(2):
                for kc in range(8):
                    tk = mm(ps[:, bb + half, :], wsl[:, kc, col0:col0 + 128],
                            xT[:, kc, th * 1024 + half * 512: th * 1024 + (half + 1) * 512],
                            kc == 0, kc == 7, [dd, tw] if (half == 0 and kc == 0) else (),
                            inc=(half == 1 and kc == 7))
            o = dst[:, th * 1024:(th + 1) * 1024]
            if evi % 2 == 0:
                te = act.run(lambda: nc.scalar.copy(o, bank2(bb)), [tk])
            else:
                te = dve.run(lambda: nc.vector.tensor_copy(o, bank2(bb)), [tk])
            evi += 1
            banks.release(bb, te, 2)
        return te

    for c in range(4):
        proj_fm(wq, c * 128, t_wq, qT[:, c, :])
    wqm = V(R4, 16640, [128, 8, 512], BF16)
    s_wqm = new_slot("wqm")
    t_wqm = dma(pool, wqm, w_in_v[:, :, 1792:2304], s_wqm, [pe.tok()])
    barrier()
    if stop_after == "b2":
        finish(nc, locals())
        return result()

    vaug = V(R2, 24 * KB, [128, 16, 2, 128], BF16)
    kTz = V(R4, 0, [128, 2, 2, T], BF16)
    va_flat = vaug.rearrange("p t g c -> p t (g c)")
    t_vones = pool.run(lambda: nc.gpsimd.memset(va_flat[:, :, 64:192], 1.0))
    t_kz0 = dve.run(lambda: nc.vector.memset(kTz[64:128, 0, :, :], 0.0))
    t_kz1 = dve.run(lambda: nc.vector.memset(kTz[0:64, 1, :, :], 0.0))
    wmk = [V(R3, 16 * KB + i * 8 * KB, [128, 8, 512], BF16) for i in range(2)]
    w_mkv_v = w_mkv_d.rearrange("(kc p) n -> p kc n", p=128)
    s_wmk = [new_slot("wmk") for _ in range(2)]
    t_wmk = [dma(pool, wmk[i], w_mkv_v[:, :, i * 512:(i + 1) * 512], s_wmk[i]) for i in range(2)]


    qmT = V(R2, 32 * KB, [128, 4, T], BF16)
    for g in range(2):
        for th in range(2):
            bb, dd = banks.alloc(2)
            for half in range(2):
                for kc in range(8):
                    tk = mm(ps[:, bb + half, :], wkv[:, kc, g * 128:(g + 1) * 128],
                            xT[:, kc, th * 1024 + half * 512: th * 1024 + (half + 1) * 512],
                            kc == 0, kc == 7, [dd, t_wkv] if (half == 0 and kc == 0) else (),
                            inc=(half == 1 and kc == 7))
            te0 = act.run(lambda: nc.scalar.copy(kTz[0:64, 0, g, th * 1024:(th + 1) * 1024], bank2(bb)[0:64, :]), [tk, t_kz0, t_kz1])
            te1 = dve.run(lambda: nc.vector.tensor_copy(kTz[64:128, 1, g, th * 1024:(th + 1) * 1024], bank2(bb)[64:128, :]), [tk, t_kz0, t_kz1])
            banks.release(bb, te0, 2)
            banks.release(bb, te1, 2)
    for tg in range(4):
        bb, dd = banks.alloc(1)
        for i4 in range(4):
            tb = tg * 4 + i4
            for kc in range(8):
                tk = mm(ps[:, bb, i4 * 128:(i4 + 1) * 128], xT[:, kc, tb * 128:(tb + 1) * 128],
                        wkv[:, kc, 256:384], kc == 0, kc == 7,
                        [dd, t_wkv] if (i4 == 0 and kc == 0) else (), inc=(i4 == 3 and kc == 7))
        o = bass.AP(vaug.tensor, vaug[:, tg * 4, 0, 0:1].offset,
                    [[vaug.ap[0][0], 128], [256, 4], [192, 2], [1, 64]])
        i_ = ps[:, bb, :].rearrange("p (t g c) -> p t g c", t=4, g=2)
        te = dve.run(lambda: nc.vector.tensor_copy(o, i_), [tk, t_vones])
        banks.release(bb, te)
    Btab = V(R5, 12 * KB, [128, 2, 3, 8, 128], BF16)
    Brev = V(R3, 32 * KB, [128, 2, 3, 8, 128], BF16)
    oh = V(R5, 0, [32, 512], F32)
    rv = V(R5, 2 * KB, [8, 512], F32)
    ge = V(R5, 4 * KB, [8, 512], F32)
    gneg = V(R5, 6 * KB, [8, 512], F32)
    hl = V(R5, 8 * KB, [8, 2, 512], F32)
    ghb = V(R5, 0, [8, 512], BF16)
    s_oh = new_slot("oh")
    dma(sp, oh, onehot_d, s_oh)
    t_oh = dma(sp, rv, relvalid_d.partition_broadcast(8).rearrange("p a b -> p (a b)"), s_oh)
    bM, dM = banks.alloc(1)
    t_g = mm(ps[0:8, bM, :], rb33[:], oh, True, True, [t_oh, t_small, dM], inc=True)
    t_b0 = dve.run(lambda: nc.vector.tensor_scalar(out=ge, in0=ps[0:8, bM, :], scalar1=8.0, scalar2=None, op0=ALU.mult), [t_g])
    banks.release(bM, t_b0)
    t_b1 = dve.run(lambda: nc.vector.tensor_tensor(out=ge, in0=ge, in1=rv, op=ALU.mult), [t_b0])
    t_b2 = dve.run(lambda: nc.vector.tensor_scalar(out=gneg, in0=rv, scalar1=800.0, scalar2=-800.0,
                                                   op0=ALU.mult, op1=ALU.add), [t_oh])
    t_b3 = dve.run(lambda: nc.vector.tensor_tensor(out=ge, in0=ge, in1=gneg, op=ALU.add), [t_b1, t_b2])
    t_b4 = dve.run(lambda: nc.vector.tensor_copy(ghb, ge), [t_b3, t_g])
    t_b5 = dve.run(lambda: nc.vector.tensor_copy(hl[:, 0, :], ghb), [t_b4])
    t_b6 = dve.run(lambda: nc.vector.tensor_tensor(out=hl[:, 1, :], in0=ge, in1=hl[:, 0, :], op=ALU.subtract), [t_b5])
    s_ge = new_slot("ge")
    t_gst = dma(sp, gexp_d.ap().rearrange("a p n -> p a n"), hl, s_ge, [t_b6])
    s_mt = new_slot("mt")
    for t in range(2):
        for kb in range(3):
            for h2 in range(2):
                src = bass.AP(gexp_d, h2 * 4096 + kb * 128 + t * 64, [[1, 64], [512, 8], [1, 128]])
                t_brev = dma(pool, Brev[h2 * 64:(h2 + 1) * 64, t, kb, :, :], src, s_mt, [t_gst])

    for c in range(4):
        proj_fm(wqm, c * 128, t_wqm, qmT[:, c, :])
    act.run(lambda: nc.scalar.activation(stt[:, 3, 2:3], eps_t[:, 0:1], AF.Exp))
    barrier()
    if stop_after == "a2":
        finish(nc, locals())
        return result()

    wgs0 = V(R5, 0, [128, 3, 8, 256], BF16)
    s_wg0 = [new_slot("wg0") for _ in range(3)]
    t_wg0 = [dma(pool, wgs0[:, b, :, :], w_in_v[:, :, 2304 + b * 1024: 2304 + b * 1024 + 256], s_wg0[b]) for b in range(3)]

    omT = V(R3, 32 * KB, [128, 4, T], BF16)
    kmT = V(R2, 52 * KB, [128, 4, NMEM], BF16)
    vm129 = V(R2, 54 * KB, [128, 2, 4, 129], BF16)
    PTm = [V(R2, 57 * KB + i * 2 * KB, [128, 2, 512], BF16) for i in range(2)]
    omtm = [V(R2, 61 * KB + i * KB, [128, 4, 128], BF16) for i in range(2)]
    t_vm1 = pool.run(lambda: nc.gpsimd.memset(vm129[:, :, :, 128:129], 1.0))
    for hp in range(2):
        bb, dd = banks.alloc(1)
        for hh in range(2):
            h = hp * 2 + hh
            for kc in range(8):
                tk = mm(ps[:, bb, hh * 256:(hh + 1) * 256], wmk[0][:, kc, h * 128:(h + 1) * 128], memT[:, kc, :],
                        kc == 0, kc == 7, [dd, t_wmk[0]] if (hh == 0 and kc == 0) else (), inc=(hh == 1 and kc == 7))
        te = dve.run(lambda: nc.vector.tensor_copy(
            kmT[:, hp * 2:hp * 2 + 2, :], ps[:, bb, :].rearrange("p (a b) -> p a b", a=2)), [tk])
        banks.release(bb, te)
    t_kmT = te
    for mc in range(2):
        bb, dd = banks.alloc(1)
        for kc in range(8):
            tk = mm(ps[:, bb, :], memT[:, kc, mc * 128:(mc + 1) * 128], wmk[1][:, kc, :],
                    kc == 0, kc == 7, [dd, t_wmk[1]] if kc == 0 else (), inc=(kc == 7))
        te = dve.run(lambda: nc.vector.tensor_copy(
            vm129[:, mc, :, 0:128], ps[:, bb, :].rearrange("p (a b) -> p a b", a=4)), [tk])
        banks.release(bb, te)
    t_vm = [te, t_vm1]
    t_btab = None
    for h2 in range(2):
        for kb in range(3):
            src = bass.AP(Brev.tensor, Brev[:, h2, kb, 0, 127:128].offset, [[Brev.ap[0][0], 128], [128, 8], [-1, 128]])
            t_btab = dve.run(lambda: nc.vector.tensor_copy(Btab[:, h2, kb, :, :], src), [t_brev])
    PTm_free = [[], []]
    om_free = [[], []]
    m_state = {}
    scale_m = 1.0 / math.sqrt(128.0)
    munits = [(tt, h) for tt in range(4) for h in range(4)]

    def m_S(u):
        tt, h = munits[u]
        r = u % 2
        bS, dS = banks.alloc(2)
        for mc in range(2):
            tk = mm(ps[:, bS + mc, :], kmT[:, h, mc * 128:(mc + 1) * 128], qmT[:, h, tt * 512:(tt + 1) * 512],
                    True, True, [dS, t_kmT] if mc == 0 else (), inc=(mc == 1))
        t_e = act.run(lambda: nc.scalar.activation(
            PTm[r].rearrange("p a b -> p (a b)"), bank2(bS), AF.Exp, scale=scale_m), [tk, PTm_free[r]])
        banks.release(bS, t_e, 2)
        m_state["S", u] = t_e

    def m_V(u):
        tt, h = munits[u]
        r = u % 2
        t_e = m_state.pop(("S", u))
        bV, dV = banks.alloc(2)
        for tbl in range(4):
            for mc in range(2):
                tk = mm(ps[:, bV + tbl // 2, (tbl % 2) * 256:(tbl % 2) * 256 + 129],
                        PTm[r][:, mc, tbl * 128:(tbl + 1) * 128], vm129[:, mc, h, :], mc == 0, mc == 1,
                        [t_e, dV, t_vm] if (tbl == 0 and mc == 0) else (), inc=(tbl == 3 and mc == 1))
        PTm_free[r] = [tk]
        dn = dnt[:, u % 2, :]
        pst = ps[:, bV:bV + 2, :].tensor
        base = ps[:, bV, 0:1].offset
        pstr = ps[:].ap[0][0]
        den = bass.AP(pst, base + 128, [[pstr, 128], [512, 2], [256, 2]])
        num = bass.AP(pst, base, [[pstr, 128], [512, 2], [256, 2], [1, 128]])
        t_rc = dve.run(lambda: nc.vector.reciprocal(dn[:, 0:4].rearrange("p (a b) -> p a b", a=2), den), [tk])
        rd_b = bass.AP(dnt, dn[:, 0:1].offset, [[dnt[:].ap[0][0], 128], [2, 2], [1, 2], [0, 128]])
        t_o = dve.run(lambda: nc.vector.tensor_tensor(
            out=omtm[r].rearrange("p (a b) c -> p a b c", a=2), in0=num, in1=rd_b, op=ALU.mult), [t_rc, om_free[r]])
        banks.release(bV, t_o, 2)
        m_state["V", u] = t_o

    def m_T(u):
        tt, h = munits[u]
        r = u % 2
        t_o = m_state.pop(("V", u))
        bT, dT = banks.alloc(1)
        for tbl in range(4):
            t_tr = pe.run(lambda: nc.tensor.transpose(
                psb[:, bT, tbl * 128:(tbl + 1) * 128], omtm[r][:, tbl, :], ident_bf[:]),
                [t_o, dT] if tbl == 0 else (), inc=(tbl == 3))
        om_free[r] = [t_tr]
        if u % 2 == 0:
            t_ev = dve.run(lambda: nc.vector.tensor_copy(omT[:, h, tt * 512:(tt + 1) * 512], psb[:, bT, 0:512]), [t_tr, t_btab])
        else:
            t_ev = act.run(lambda: nc.scalar.copy(omT[:, h, tt * 512:(tt + 1) * 512], psb[:, bT, 0:512]), [t_tr, t_btab])
        banks.release(bT, t_ev)

    NM = len(munits)
    m_S(0)
    for u in range(NM + 1):
        if u + 1 < NM:
            m_S(u + 1)
        if u < NM:
            m_V(u)
        if u >= 1:
            m_T(u - 1)
    barrier()
    if stop_after == "b1":
        finish(nc, locals())
        return result()

    wos0 = V(R2, 56 * KB, [128, 3, 4, 256], BF16)
    s_wos0 = [new_slot("wos0") for _ in range(3)]
    w_out_v0 = [w_co_d.rearrange("(kc p) n -> p kc n", p=128),
                w_ao_d.rearrange("(kc p) n -> p kc n", p=128),
                w_mo_d.rearrange("(kc p) n -> p kc n", p=128)]
    t_wos0 = [dma(pool, wos0[:, b, :, :], w_out_v0[b][:, :, 0:256], s_wos0[b]) for b in range(3)]

    oT = V(R3, 16 * KB, [128, 4, T], BF16)
    otm = [V(R4, 16 * KB + i * 512, [128, 4, 64], BF16) for i in range(3)]
    PTw = [V(R4, 22 * KB + i * KB, [128, 512], BF16) for i in range(6)]
    otm_free = [[] for _ in range(3)]
    PT_free = [[] for _ in range(6)]
    w_state = {}
    pcount = [0]
    units = [(n, g) for n in range(16) for g in range(2)]

    def w_S(u):
        n, g = units[u]
        kbs = [kb for kb in range(3) if 0 <= n + kb - 1 < 16]
        res = []
        for kb in kbs:
            kblk = n + kb - 1
            bS, dS = banks.alloc(1)
            for half in range(2):
                mm(ps[:, bS, half * 256:(half + 1) * 256].rearrange("p (a b) -> p a b", a=2),
                   kTz[:, half, g, kblk * 128:(kblk + 1) * 128],
                   qT[:, 2 * g:2 * g + 2, n * 128:(n + 1) * 128],
                   half == 0, False, [dS] if half == 0 else ())
            for t in range(2):
                tk = mm(ps[t * 64:(t + 1) * 64, bS, :], ident2[:],
                        Btab[:, t, kb, g * 4:(g + 1) * 4, :].rearrange("p a b -> p (a b)"),
                        False, True, [t_btab, t_id2a, t_id2] if t == 0 else (), inc=(t == 1))
            rp = pcount[0] % 6
            pcount[0] += 1
            t_p = act.run(lambda: nc.scalar.activation(PTw[rp], ps[:, bS, :], AF.Exp, scale=0.125), [tk, PT_free[rp]])
            banks.release(bS, t_p)
            res.append((kb, rp, t_p))
        w_state["S", u] = res

    def w_V(u):
        n, g = units[u]
        res = w_state.pop(("S", u))
        bO, dO = banks.alloc(1)
        c0 = 0 if g == 0 else 63
        for s4 in range(4):
            for i, (kb, rp, t_p) in enumerate(res):
                kblk = n + kb - 1
                tk = mm(ps[:, bO, s4 * 128: s4 * 128 + 65], PTw[rp][:, s4 * 128:(s4 + 1) * 128],
                        vaug[:, kblk, g, c0:c0 + 65], i == 0, i == len(res) - 1,
                        [t_p, dO] if s4 == 0 else (), inc=(s4 == 3 and i == len(res) - 1))
        for (kb, rp, t_p) in res:
            PT_free[rp] = [tk]
        ncol, dcol = (0, 64) if g == 0 else (1, 0)
        r3 = u % 3
        dn = dnt[:, u % 2, :]
        pv = ps[:, bO, :].rearrange("p (a b) -> p a b", a=4)
        t_dn = dve.run(lambda: nc.vector.tensor_tensor(
            out=dn[:, 0:4], in0=pv[:, :, dcol], in1=esink8[:, g * 4:(g + 1) * 4], op=ALU.add), [tk, t_esink])
        t_rc = dve.run(lambda: nc.vector.reciprocal(dn[:, 4:8], dn[:, 0:4]), [t_dn])
        rd_b = bass.AP(dnt, dn[:, 4:5].offset, [[dnt[:].ap[0][0], 128], [1, 4], [0, 64]])
        t_o = dve.run(lambda: nc.vector.tensor_tensor(
            out=otm[r3], in0=pv[:, :, ncol:ncol + 64], in1=rd_b, op=ALU.mult), [t_rc, otm_free[r3]])
        banks.release(bO, t_o)
        w_state["V", u] = (r3, t_o)

    def w_T(u):
        n, g = units[u]
        r3, t_o = w_state.pop(("V", u))
        if g == 0:
            w_state["bT", n] = banks.alloc(1)
        bT, dT = w_state["bT", n]
        of = otm[r3].rearrange("p a b -> p (a b)")
        for sp2 in range(2):
            c4 = g * 2 + sp2
            t_tr = pe.run(lambda: nc.tensor.transpose(
                psb[:, bT, c4 * 128:(c4 + 1) * 128], of[:, sp2 * 128:(sp2 + 1) * 128], ident_bf[:]),
                [t_o, dT] if sp2 == 0 else (), inc=(sp2 == 1))
        otm_free[r3] = [t_tr]
        if g == 1:
            src = psb[:, bT, 0:512].rearrange("p (a b) -> p a b", a=4)
            t_ev = dve.run(lambda: nc.vector.tensor_copy(oT[:, :, n * 128:(n + 1) * 128], src), [t_tr])
            banks.release(bT, t_ev)
            del w_state["bT", n]

    NU = len(units)
    w_S(0)
    for u in range(NU + 1):
        if u + 1 < NU:
            w_S(u + 1)
        if u < NU:
            w_V(u)
        if u >= 1:
            w_T(u - 1)
    act.run(lambda: nc.scalar.activation(stt[:, 3, 3:4], eps_t[:, 0:1], AF.Sigmoid))
    barrier()
    if stop_after == "b3":
        finish(nc, locals())
        return result()

    merged = V(R4, 0, [128, 8, T], BF16)
    wgs = [V(R2, i * 18 * KB, [128, 3, 8, 256], BF16) for i in range(2)]
    wos = [V(R2, i * 18 * KB + 12 * KB, [128, 3, 4, 256], BF16) for i in range(2)]
    sgb = [V(R2, 36 * KB + i * 2 * KB, [128, 512], F32) for i in range(6)]
    mb = [V(R2, 48 * KB + i * 2 * KB, [128, 512], F32) for i in range(4)]
    s_wc = [[new_slot("wc") for _ in range(6)] for _ in range(2)]
    wc_free = [[], []]
    w_out_v = [w_co_d.rearrange("(kc p) n -> p kc n", p=128),
               w_ao_d.rearrange("(kc p) n -> p kc n", p=128),
               w_mo_d.rearrange("(kc p) n -> p kc n", p=128)]

    def load_wc(dp):
        r = dp % 2
        toks = {}
        for b in range(3):
            if dp == 0:
                toks["g", b] = t_wg0[b]
            else:
                toks["g", b] = dma(pool, wgs[r][:, b, :, :],
                                   w_in_v[:, :, 2304 + b * 1024 + dp * 256: 2304 + b * 1024 + (dp + 1) * 256],
                                   s_wc[r][b], wc_free[r] if b == 0 else ())
            if dp == 0:
                toks["o", b] = t_wos0[b]
            else:
                toks["o", b] = dma(pool, wos[r][:, b, :, :], w_out_v[b][:, :, dp * 256:(dp + 1) * 256], s_wc[r][3 + b],
                                   wc_free[r] if b == 0 else ())
        return toks

    srcs = None
    sg_free = [[] for _ in range(6)]
    m_free = [[] for _ in range(4)]
    sgi = mi = 0
    t_wc = {0: load_wc(0)}
    w_o_sb = V(R5, 0, [128, 8, D], BF16)
    ln1t = V(R5, 16 * KB, [128, 2, D], F32)
    s_ln1 = new_slot("ln1")
    t_ln1 = dma(sp, ln1t, bass.AP(ln1_d.tensor, 0, [[0, 128], [D, 2], [1, D]]), s_ln1)
    s_wo = new_slot("wo")
    for dp in range(4):
        if dp + 1 < 4:
            t_wc[dp + 1] = load_wc(dp + 1)
        if dp == 1:
            t_wo = dma(pool, w_o_sb, w_o_d.rearrange("(kc p) n -> p kc n", p=128), s_wo, [wg0_done])
        r = dp % 2
        wg_cur = wgs0 if dp == 0 else wgs[r]
        wo_cur = wos0 if dp == 0 else wos[r]
        last_pe = None
        for tt in range(4):
            tok_sl = slice(tt * 512, (tt + 1) * 512)
            for dci in range(2):
                dc = dp * 2 + dci
                cs = slice(dci * 128, (dci + 1) * 128)
                bra = [sT, oT, omT]
                t_sg = [None] * 3
                sg_r = [None] * 3
                bY = [None] * 3
                tY = [None] * 3
                for b in range(3):
                    bG, dG = banks.alloc(1)
                    for kc in range(8):
                        tk = mm(ps[:, bG, :], wg_cur[:, b, kc, cs], xT[:, kc, tok_sl], kc == 0, kc == 7,
                                [dG, t_wc[dp]["g", b]] if kc == 0 else (), inc=(kc == 7))
                    sr = sgi % 6
                    sgi += 1
                    t_sg[b] = act.run(lambda b=b: nc.scalar.activation(
                        sgb[sr], ps[:, bG, :], AF.Sigmoid, bias=bgate[:, b * 8 + dc: b * 8 + dc + 1]),
                        [tk, sg_free[sr]])
                    sg_r[b] = sr
                    banks.release(bG, t_sg[b])
                    bY[b], dY = banks.alloc(1)
                    for kc in range(4):
                        tY[b] = mm(ps[:, bY[b], :], wo_cur[:, b, kc, cs], bra[b][:, kc, tok_sl], kc == 0, kc == 3,
                                   [dY, t_wc[dp]["o", b]] if kc == 0 else (), inc=(kc == 3))
                    last_pe = tY[b]
                m0 = mi % 4
                m1 = (mi + 1) % 4
                mi += 2
                t0 = dve.run(lambda: nc.vector.tensor_tensor(out=mb[m0], in0=ps[:, bY[0], :], in1=sgb[sg_r[0]], op=ALU.mult),
                             [tY[0], t_sg[0], m_free[m0]])
                banks.release(bY[0], t0)
                t1 = dve.run(lambda: nc.vector.tensor_tensor(out=mb[m1], in0=ps[:, bY[1], :], in1=sgb[sg_r[1]], op=ALU.mult),
                             [tY[1], t_sg[1], m_free[m1]])
                banks.release(bY[1], t1)
                sg_free[sg_r[0]] = [t0]
                sg_free[sg_r[1]] = [t1]
                t2 = dve.run(lambda: nc.vector.tensor_tensor(out=mb[m0], in0=mb[m0], in1=mb[m1], op=ALU.add), [t0, t1])
                t3 = dve.run(lambda: nc.vector.tensor_tensor(out=mb[m1], in0=ps[:, bY[2], :], in1=sgb[sg_r[2]], op=ALU.mult),
                             [tY[2], t_sg[2], t2])
                banks.release(bY[2], t3)
                sg_free[sg_r[2]] = [t3]
                t4 = dve.run(lambda: nc.vector.tensor_tensor(out=merged[:, dc, tok_sl], in0=mb[m0], in1=mb[m1], op=ALU.add),
                             [t2, t3])
                m_free[m0] = [t4]
                m_free[m1] = [t4]
        wc_free[r] = [last_pe]
        if dp == 0:
            wg0_done = last_pe
    barrier()
    if stop_after == "c":
        finish(nc, locals())
        return result()

    x1 = V(R2, 0, [128, 16, D], F32)
    x1T = V(R1, 0, [128, 8, T], BF16)
    x1b = [V(R3, 44 * KB + i * 2 * KB, [128, D], BF16) for i in range(2)] + [x1b_extra[:]]
    w2 = V(R3, 0, [128, 22, D], BF16)
    s_w2 = new_slot("w2")
    w_f2_v = w_f2_d.rearrange("(j p) n -> p j n", p=128)
    s_x = [new_slot("xr") for _ in range(16)]
    t_x = [dma(sp, x1[:, tb, :], x_d[tb * 128:(tb + 1) * 128, :], s_x[tb]) for tb in range(16)]
    x1b_free = [[], [], []]

    def ln_stats(buf, deps, k, junk):
        st = stt[:, k % 4, :]
        tb_ = act.run(lambda: nc.scalar.activation(junk, buf, AF.Square, accum_out=st[:, 1:2]), deps)
        return tb_

    def ln_tiny(t_st, k, sum_ap):
        st = stt[:, k % 4, :]
        mv = mvt[:, k % 4, :]
        t1 = dve.run(lambda: nc.vector.tensor_scalar(
            out=mv[:, 0:1], in0=sum_ap, scalar1=1.0 / D, scalar2=None, op0=ALU.mult), [t_st])
        t2 = dve.run(lambda: nc.vector.tensor_tensor(out=mv[:, 1:2], in0=mv[:, 0:1], in1=mv[:, 0:1], op=ALU.mult), [t1])
        t3 = dve.run(lambda: nc.vector.scalar_tensor_tensor(
            out=mv[:, 1:2], in0=st[:, 1:2], scalar=1.0 / D, in1=mv[:, 1:2], op0=ALU.mult, op1=ALU.subtract), [t2])
        return t3

    def ln_sqrt(t3, k):
        mv = mvt[:, k % 4, :]
        return act.run(lambda: nc.scalar.activation(mv[:, 2:3], mv[:, 1:2], AF.Sqrt, bias=eps_t[:, 0:1]), [t3])

    def ln_recip(td0, k):
        mv = mvt[:, k % 4, :]
        return dve.run(lambda: nc.vector.reciprocal(mv[:, 2:3], mv[:, 2:3]), [td0])

    def ln_rstd(t_st, k, sum_ap):
        return ln_recip(ln_sqrt(ln_tiny(t_st, k, sum_ap), k), k)

    def ln_affine(buf, tabs, dst, t_rs, k, deps=()):
        mv = mvt[:, k % 4, :]
        te = dve.run(lambda: nc.vector.scalar_tensor_tensor(
            out=buf, in0=buf, scalar=mv[:, 0:1], in1=tabs[:, 0, :], op0=ALU.subtract, op1=ALU.mult), [t_rs, deps])
        tf = dve.run(lambda: nc.vector.scalar_tensor_tensor(
            out=dst, in0=buf, scalar=mv[:, 2:3], in1=tabs[:, 1, :], op0=ALU.mult, op1=ALU.add), [te])
        return tf

    def d_mm(tb):
        bO, dO = banks.alloc(2)
        for half in range(2):
            for kc in range(8):
                tk = mm(ps[:, bO + half, :], merged[:, kc, tb * 128:(tb + 1) * 128],
                        w_o_sb[:, kc, half * 512:(half + 1) * 512], kc == 0, kc == 7,
                        [dO, t_wo] if (half == 0 and kc == 0) else (), inc=(half == 1 and kc == 7))
        return bO, tk

    d_state = {}
    t_x1T = [None] * 16
    d_state_last_mm = [None]

    def d_H(tb):
        bO, tk = d_state.pop(("mm", tb))
        xb = x1[:, tb, :]
        t_h = dve.run(lambda: nc.vector.scalar_tensor_tensor(
            out=xb, in0=xb, scalar=ALPHA, in1=bank2(bO), op0=ALU.mult, op1=ALU.add,
            accum_out=hsum[:, 0, tb:tb + 1]), [tk, t_x[tb]])
        banks.release(bO, t_h, 2)
        d_state["h", tb] = t_h

    def d_S(tb):
        xb = x1[:, tb, :]
        r = tb % 3
        d_state["st", tb] = ln_stats(xb, [d_state.pop(("h", tb)), x1b_free[r]], tb, x1b[r])

    def d_tiny(tb):
        d_state["t3", tb] = ln_tiny(d_state.pop(("st", tb)), tb, hsum[:, 0, tb:tb + 1])

    def d_sqrt(tb):
        d_state["sq", tb] = ln_sqrt(d_state.pop(("t3", tb)), tb)

    def d_recip(tb):
        d_state["rs", tb] = ln_recip(d_state.pop(("sq", tb)), tb)

    def d_C(tb):
        xb = x1[:, tb, :]
        r = tb % 3
        t_ln = ln_affine(xb, ln1t, xb, d_state.pop(("rs", tb)), tb, [t_ln1])
        d_state["cb", tb] = act.run(lambda: nc.scalar.copy(x1b[r], xb), [t_ln])

    def d_Tpe(tb):
        r = tb % 3
        t_cb = d_state.pop(("cb", tb))
        bT, dT = banks.alloc(1)
        for kc in range(8):
            t_tr = pe.run(lambda kc=kc: nc.tensor.transpose(
                psb[:, bT, kc * 128:(kc + 1) * 128], x1b[r][:, kc * 128:(kc + 1) * 128], ident_bf[:]),
                [t_cb, dT] if kc == 0 else (), inc=(kc == 7))
        x1b_free[r] = [t_tr]
        d_state["tr", tb] = (bT, t_tr)

    def d_Tev(tb):
        bT, t_tr = d_state.pop(("tr", tb))
        t_ev = act.run(lambda: nc.scalar.copy(
            x1T[:, :, tb * 128:(tb + 1) * 128], psb[:, bT, :].rearrange("p (a b) -> p a b", a=8)),
            [t_tr, e_state.get("tmp_slab_done") if tb >= 12 else None])
        banks.release(bT, t_ev)
        t_x1T[tb] = t_ev

    actT = V(R4, 0, [128, 22, 512], BF16)
    ln2t = V(R4, 22 * KB, [128, 2, D], F32)
    sgf = [V(R4, 30 * KB, [128, 512], F32), sgf_extra[:]]
    w1s = [V(R5, i * 8 * KB, [128, 2, 8, 256], BF16) for i in range(2)]
    ost = [V(R5, 16 * KB + i * 4 * KB, [128, D], F32) for i in range(2)]
    s_ln2 = new_slot("ln2")
    w_f1_v = w1b_d.rearrange("(kc p) n -> p kc n", p=128)
    s_w1 = [new_slot("w1") for _ in range(2)]
    w1_free = [[], []]
    w1_i = [0]

    def load_w1(jp):
        r = w1_i[0] % 2
        w1_i[0] += 1
        dma(sp, w1s[r][:, 0, :, :], w_f1_v[:, :, jp * 256:(jp + 1) * 256], s_w1[r], [w1_free[r], t_w1c])
        return r, dma(sp, w1s[r][:, 1, :, :], w_f1_v[:, :, DFF + jp * 256: DFF + (jp + 1) * 256], s_w1[r])

    sg_free = [[], []]
    ost_free = [[], []]
    s_out = [new_slot("out") for _ in range(2)]
    seq = [(Q, jp) for Q in range(4) for jp in range(11)]
    e_state = {"sgi": 0, "oi": 0}

    w1s.append(bass.AP(x1T.tensor, x1T[:, 0, 1536:1537].offset, [[x1T.ap[0][0], 128], [256, 2], [T, 8], [1, 256]]))
    s_w1.append(new_slot("w1t"))
    dma(sp, w1s[2][:, 0, :, :], w_f1_v[:, :, 0:256], s_w1[2], [t_w1c])
    t_w1_first = dma(sp, w1s[2][:, 1, :, :], w_f1_v[:, :, DFF:DFF + 256], s_w1[2])

    def e_start(dep):
        e_state["t_ln2"] = dma(sp, ln2t, bass.AP(ln2_d.tensor, 0, [[0, 128], [D, 2], [1, D]]), s_ln2, [dep])
        w1_free[0] = [dep]
        w1_free[1] = [dep]
        e_state["pending"] = (2, t_w1_first)

    def e_step(si):
        Q, jp = seq[si]
        sgi = e_state["sgi"]
        oi = e_state["oi"]
        t_ln2 = e_state["t_ln2"]
        r, t_w1 = e_state["pending"]
        if si + 1 < len(seq):
            e_state["pending"] = load_w1(seq[si + 1][1])
        tok_sl = slice(Q * 512, (Q + 1) * 512)
        for jj in range(2):
            j = jp * 2 + jj
            cs = slice(jj * 128, (jj + 1) * 128)
            bG, dG = banks.alloc(1)
            for kc in range(8):
                tG = mm(ps[:, bG, :], w1s[r][:, 0, kc, cs], x1T[:, kc, tok_sl], kc == 0, kc == 7,
                        [dG, t_w1, t_x1T[4 * Q:4 * Q + 4]] if kc == 0 else (), inc=(kc == 7))
            bU, dU = banks.alloc(1)
            for kc in range(8):
                tU = mm(ps[:, bU, :], w1s[r][:, 1, kc, cs], x1T[:, kc, tok_sl], kc == 0, kc == 7,
                        [dU] if kc == 0 else (), inc=(kc == 7))
            sr = sgi % 2
            sgi += 1
            t_s = act.run(lambda: nc.scalar.activation(sgf[sr], ps[:, bG, :], AF.Silu), [tG, sg_free[sr]])
            banks.release(bG, t_s)
            t_a = dve.run(lambda: nc.vector.tensor_tensor(out=actT[:, j, :], in0=ps[:, bU, :], in1=sgf[sr], op=ALU.mult),
                          [tU, t_s])
            banks.release(bU, t_a)
            sg_free[sr] = [t_a]
        if r < 2:
            w1_free[r] = [tU]
        else:
            e_state["tmp_slab_done"] = tU
        if jp == 10:
            for tbl in range(4):
                tb = Q * 4 + tbl
                bO, dO = banks.alloc(2)
                for half in range(2):
                    for j in range(22):
                        tk = mm(ps[:, bO + half, :], actT[:, j, tbl * 128:(tbl + 1) * 128],
                                w2[:, j, half * 512:(half + 1) * 512], j == 0, j == 21,
                                [dO, t_w2, t_a] if (half == 0 and j == 0) else (), inc=(half == 1 and j == 21))
                xb = x1[:, tb, :]
                t_h = dve.run(lambda: nc.vector.scalar_tensor_tensor(
                    out=xb, in0=xb, scalar=ALPHA, in1=bank2(bO), op0=ALU.mult, op1=ALU.add,
                    accum_out=hsum[:, 1, tb:tb + 1]), [tk])
                banks.release(bO, t_h, 2)
                ro = oi % 2
                oi += 1
                t_st2 = ln_stats(xb, [t_h, ost_free[ro]], tb, ost[ro])
                t_rs2 = ln_rstd(t_st2, tb, hsum[:, 1, tb:tb + 1])
                t_ln = ln_affine(xb, ln2t, ost[ro], t_rs2, tb, [t_ln2])
                t_st = dma(pool, out_d[tb * 128:(tb + 1) * 128, :], ost[ro], s_out[ro], [t_ln])
                ost_free[ro] = [t_st]
        e_state["sgi"] = sgi
        e_state["oi"] = oi

    for tb in range(9):
        d_state["mm", tb] = d_mm(tb)
        d_state_last_mm[0] = d_state["mm", tb][1]
        if tb >= 1:
            d_H(tb - 1)
    for tb in range(2):
        d_S(tb)
        d_tiny(tb)
    d_sqrt(0)
    d_recip(0)
    for tb in range(17):
        if tb >= 1:
            d_Tpe(tb - 1)
        if tb + 9 < 16:
            d_state["mm", tb + 9] = d_mm(tb + 9)
            d_state_last_mm[0] = d_state["mm", tb + 9][1]
        if tb == 6:
            dma(pool, w2[:, 0:11, :], w_f2_v[:, 0:11, :], s_w2)
            t_w2 = dma(pool, w2[:, 11:22, :], w_f2_v[:, 11:22, :], s_w2)
        if tb + 1 < 16:
            d_sqrt(tb + 1)
        if tb >= 1:
            d_Tev(tb - 1)
        if tb + 2 < 16:
            d_S(tb + 2)
        if tb < 16:
            d_C(tb)
        if tb + 1 < 16:
            d_recip(tb + 1)
        if tb + 8 < 16:
            d_H(tb + 8)
        if tb + 2 < 16:
            d_tiny(tb + 2)
        if tb == 6:
            e_start(d_state_last_mm[0])
        if tb >= 7:
            e_step(tb - 7)
    n_e_done = 10
    if stop_after == "d":
        barrier()
        finish(nc, locals())
        return result()
    for si in range(n_e_done, len(seq)):
        e_step(si)
    pool.wait([s.tok() for s in s_out])
    return result()


def finish(nc, env):
    taps = env["taps"]
    sp = env["sp"]
    slot = env["new_slot"]("tap")
    for name, getter in taps.items():
        ap, shape, dt = getter(env)
        o = nc.dram_tensor("tap_" + name, list(shape), dt, kind="ExternalOutput").ap()
        sp.eng.dma_start(out=o, in_=ap).then_inc(slot.sem, 16)
        slot.count += 16
    sp.wait([slot.tok()])
    return nc


def _t5_bucket_np(rel):
    half, max_exact = 16, 8
    base = np.where(rel > 0, half, 0)
    n = np.abs(rel)
    nf = np.maximum(n, 1).astype(np.float32)
    large = max_exact + (np.log(nf / np.float32(max_exact)) / np.float32(math.log(128 / max_exact))
                         * np.float32(half - max_exact)).astype(np.int32)
    large = np.minimum(large, half - 1)
    return base + np.where(n < max_exact, n, large)


def _consts():
    j = np.arange(512)
    rel = j - 255
    valid = (np.abs(rel) <= 128) & (j < 511)
    bucket = _t5_bucket_np(rel.astype(np.int32))
    onehot = np.zeros((32, 512), np.float32)
    onehot[bucket[valid], j[valid]] = 1.0
    relvalid = valid.astype(np.float32)[None, :]
    return onehot, relvalid


def prepare_inputs(inputs):
    f = lambda a: np.ascontiguousarray(np.asarray(a, dtype=np.float32))
    onehot, relvalid = _consts()
    w_ao = np.asarray(inputs["w_attn_out"][0], np.float32)
    w_ao_p = np.concatenate(
        [w_ao[PERM[g * 4 + sp * 2 + s2] * 64:(PERM[g * 4 + sp * 2 + s2] + 1) * 64]
         for g in range(2) for sp in range(2) for s2 in range(2)], 0)
    dww = np.asarray(inputs["conv_dw_w"][0], np.float32)
    dwp = np.concatenate([dww, np.zeros((1, 512), np.float32)], 0).reshape(16, 2, 4, 2, 64)
    scE = dwp[:, :, :, 0, :].transpose(1, 3, 2, 0).reshape(128, 4, 16)
    scO = dwp[:, ::-1, :, 1, :].transpose(1, 3, 2, 0).reshape(128, 4, 16)
    dwsc = np.stack([scE, scO], 1).reshape(128, 128)
    cv = np.stack([np.asarray(inputs[k][0], np.float32).reshape(4, 128).T
                   for k in ("conv_dw_b", "conv_ln_g", "conv_ln_b")], 1).reshape(128, 12)
    shared = {
        "rbp": f(np.asarray(inputs["rel_bias"])[:, PERM]),
        "sinkp": f(np.asarray(inputs["attn_sink"][0])[PERM][None, :]),
        "onehot": onehot, "relvalid": relvalid,
        "w_in": f(inputs["w_in"][0]),
        "bgate": f(np.asarray(inputs["b_gate"][0]).reshape(24, 128).T),
        "dwsc": f(dwsc), "cvec": f(cv),
        "w_co": f(inputs["w_conv_out"][0]), "w_ao": f(w_ao_p),
        "w_mkv": f(inputs["w_mem_kv"][0]), "w_mo": f(inputs["w_mem_out"][0]),
        "w_o": f(inputs["w_o"][0]),
        "ln1": f(np.stack([np.asarray(inputs["ln1_g"][0]), np.asarray(inputs["ln1_b"][0])], 0)),
        "w_f1": f(inputs["w_ffn_in"][0]), "w_f2": f(inputs["w_ffn_out"][0]),
        "ln2": f(np.stack([np.asarray(inputs["ln2_g"][0]), np.asarray(inputs["ln2_b"][0])], 0)),
    }
    x = np.asarray(inputs["x"], np.float32)
    mem = np.asarray(inputs["mem"], np.float32)
    return [dict(shared, x=np.ascontiguousarray(x[b]), mem=np.ascontiguousarray(mem[b])) for b in range(8)]


def kernel(**inputs):
    in_maps = prepare_inputs(inputs)
    nc = build_nc()
    res = run_bass_kernel_spmd(nc, in_maps, core_ids=list(range(8)))
    return np.stack([np.asarray(r["out"], np.float32) for r in res.results], 0)
```

```python
import math
import numpy as np
import concourse.bass as bass
import concourse.mybir as mybir
from concourse.bass_utils import run_bass_kernel_spmd

F32 = mybir.dt.float32
BF16 = mybir.dt.bfloat16
ALU = mybir.AluOpType
AF = mybir.ActivationFunctionType

D = 1024
T = 2048
NMEM = 256
IN_DIM = 5376
DFF = 2816
ALPHA = 2.0 ** 0.25
LN_EPS = 1e-5
PERM = [0, 2, 1, 3, 4, 6, 5, 7]
KB = 1024

R1_B, R2_B, R3_B, R4_B, R5_B = 32 * KB, 64 * KB, 48 * KB, 32 * KB, 24 * KB


def _flat(deps):
    out = []
    for d in deps:
        if d is None:
            continue
        if isinstance(d, list):
            out.extend(_flat(d))
        elif isinstance(d, tuple) and len(d) > 0 and (d[0] == "E" or not isinstance(d[0], (tuple, list))):
            out.append(d)
        else:
            out.extend(_flat(d))
    return out


class Eng:
    def __init__(self, nc, eng, name, plan):
        self.nc, self.eng, self.name = nc, eng, name
        self.sem = nc.alloc_semaphore("sem_" + name)
        self.v = 0
        self.real = 0
        self.vmap = {}
        self.seen = {}
        self.plan = plan
        self.needed = set()

    def wait(self, deps):
        for tok in _flat(deps):
            if tok[0] == "E":
                _, src, v = tok
                k = id(src)
                if self.seen.get(k, 0) >= v:
                    continue
                self.seen[k] = v
                src.needed.add(v)
                if self.plan is not None:
                    self.eng.wait_ge(src.sem, src.vmap[v])
            else:
                sem, val = tok
                k = id(sem)
                if self.seen.get(k, 0) >= val:
                    continue
                self.eng.wait_ge(sem, val)
                self.seen[k] = val

    def run(self, fn, deps=(), inc=True):
        self.wait(deps)
        ins = fn()
        if inc:
            self.v += 1
            if self.plan is None or self.v in self.plan[self.name]:
                ins.then_inc(self.sem, 1)
                self.real += 1
                self.vmap[self.v] = self.real
            return ("E", self, self.v)
        return None

    def tok(self):
        return ("E", self, self.v)


class Slot:
    def __init__(self, nc, name):
        self.sem = nc.alloc_semaphore("dq_" + name)
        self.count = 0

    def tok(self):
        return (self.sem, self.count)


def build_nc(stop_after=None, taps=None):
    plan = _build(stop_after, taps, None)[1]
    return _build(stop_after, taps, plan)[0]


def _build(stop_after, taps, plan):
    nc = bass.Bass("TRN2", target_bir_lowering=False)
    taps = taps or {}

    def din(name, shape):
        return nc.dram_tensor(name, list(shape), F32, kind="ExternalInput").ap()

    x_d = din("x", [T, D])
    mem_d = din("mem", [NMEM, D])
    rbp_d = din("rbp", [32, 8])
    sinkp_d = din("sinkp", [1, 8])
    onehot_d = din("onehot", [32, 512])
    relvalid_d = din("relvalid", [1, 512])
    w_in_d = din("w_in", [D, IN_DIM])
    bgate_d = din("bgate", [128, 24])
    dwsc_d = din("dwsc", [128, 128])
    cvec_d = din("cvec", [128, 12])
    w_co_d = din("w_co", [512, D])
    w_ao_d = din("w_ao", [512, D])
    w_mkv_d = din("w_mkv", [D, D])
    w_mo_d = din("w_mo", [512, D])
    w_o_d = din("w_o", [D, D])
    ln1_d = din("ln1", [2, D])
    w_f1_d = din("w_f1", [D, 2 * DFF])
    w_f2_d = din("w_f2", [DFF, D])
    ln2_d = din("ln2", [2, D])
    out_d = nc.dram_tensor("out", [T, D], F32, kind="ExternalOutput").ap()
    gexp_d = nc.dram_tensor("gexp_scratch", [2, 8, 512], F32, kind="Internal")
    w1b_d = nc.dram_tensor("w1_bf16_cache", [D, 2 * DFF], BF16, kind="Internal").ap()

    pe = Eng(nc, nc.tensor, "pe", plan)
    act = Eng(nc, nc.scalar, "act", plan)
    dve = Eng(nc, nc.vector, "dve", plan)
    pool = Eng(nc, nc.gpsimd, "pool", plan)
    sp = Eng(nc, nc.sync, "sp", plan)
    engines = [pe, act, dve, pool, sp]

    def result():
        return nc, {e.name: set(e.needed) for e in engines}

    def barrier():
        toks = [e.tok() for e in engines]
        for e in engines:
            e.wait([t for t, f in zip(toks, engines) if f is not e])

    _slot_n = [0]

    def new_slot(name="s"):
        _slot_n[0] += 1
        return Slot(nc, f"{name}{_slot_n[0]}")

    def dma(e, out, in_, slot, deps=()):
        e.wait(deps)
        e.eng.dma_start(out=out, in_=in_).then_inc(slot.sem, 16)
        slot.count += 16
        return slot.tok()

    R1 = nc.alloc_sbuf_tensor("R1", [128, R1_B // 2], BF16)
    R2 = nc.alloc_sbuf_tensor("R2", [128, R2_B // 2], BF16)
    R3 = nc.alloc_sbuf_tensor("R3", [128, R3_B // 2], BF16)
    R4 = nc.alloc_sbuf_tensor("R4", [128, R4_B // 2], BF16)
    R5 = nc.alloc_sbuf_tensor("R5", [128, R5_B // 2], BF16)

    def V(reg, off, shape, dt, p0=0):
        esz = 2 if dt == BF16 else 4
        n = 1
        for s in shape[1:]:
            n *= s
        a = reg[p0:p0 + shape[0], off // 2: off // 2 + (n * esz) // 2]
        if dt != BF16:
            a = a.bitcast(dt)
        if len(shape) >= 3:
            names = "abcdefg"[:len(shape) - 1]
            pat = "p (" + " ".join(names) + ") -> p " + " ".join(names)
            a = a.rearrange(pat, **{nm: shape[i + 1] for i, nm in enumerate(names[:-1])})
        return a

    ident_bf = nc.alloc_sbuf_tensor("ident_bf", [128, 128], BF16)
    ident_f = nc.alloc_sbuf_tensor("ident_f", [128, 128], F32)
    ones_bf = nc.alloc_sbuf_tensor("ones_bf", [128, 128], BF16)
    ones_f = nc.alloc_sbuf_tensor("ones_f", [128, 128], F32)
    bgate = nc.alloc_sbuf_tensor("bgate_sb", [128, 24], F32)
    dwsc = nc.alloc_sbuf_tensor("dwsc_sb", [128, 128], F32)
    ident2 = nc.alloc_sbuf_tensor("ident2", [128, 64], BF16)
    cvec = nc.alloc_sbuf_tensor("cvec_sb", [128, 12], F32)
    esink8 = nc.alloc_sbuf_tensor("esink8", [128, 8], F32)
    rb33 = nc.alloc_sbuf_tensor("rb33", [32, 8], F32)
    stt = nc.alloc_sbuf_tensor("stt", [128, 4, 4], F32)
    mvt = nc.alloc_sbuf_tensor("mvt", [128, 4, 4], F32)
    x1b_extra = nc.alloc_sbuf_tensor("x1b_extra", [128, D], BF16)
    sgf_extra = nc.alloc_sbuf_tensor("sgf_extra", [128, 512], F32)
    hsum = nc.alloc_sbuf_tensor("hsum", [128, 2, 16], F32)
    eps_t = nc.alloc_sbuf_tensor("eps_t", [128, 1], F32)
    dnt = nc.alloc_sbuf_tensor("dnt", [128, 2, 8], F32)

    ps = nc.alloc_psum_tensor("ps", [128, 8, 512], F32)
    psb = ps[:].bitcast(BF16)

    class Banks:
        def __init__(self):
            self.free = [[] for _ in range(8)]
            self.busy = [False] * 8
            self.ptr = 0

        def alloc(self, n=1):
            if n == 2 and self.ptr % 2:
                self.ptr = (self.ptr + 1) % 8
            b = self.ptr
            self.ptr = (self.ptr + n) % 8
            deps = []
            for i in range(n):
                assert not self.busy[b + i], f"PSUM bank {b + i} re-allocated before its consumer was emitted"
                self.busy[b + i] = True
                deps += self.free[b + i]
                self.free[b + i] = []
            return b, deps

        def release(self, b, tok, n=1):
            for i in range(n):
                self.free[b + i].append(tok)
                self.busy[b + i] = False

    banks = Banks()

    def bank2(b):
        return ps[:, b:b + 2, :].rearrange("p a b -> p (a b)")

    def mm(out, lhsT, rhs, start, stop, deps=(), inc=False):
        return pe.run(lambda: nc.tensor.matmul(out, lhsT, rhs, start=start, stop=stop), deps, inc)

    w_in_v = w_in_d.rearrange("(kc p) n -> p kc n", p=128)

    s_small = new_slot("small")
    dma(sp, bgate[:], bgate_d, s_small)
    dma(sp, dwsc[:], dwsc_d, s_small)
    dma(sp, cvec[:], cvec_d, s_small)
    dma(sp, rb33[:], rbp_d, s_small)
    dma(sp, esink8[:], sinkp_d.partition_broadcast(128).rearrange("p a b -> p (a b)"), s_small)
    t_small = s_small.tok()

    t_ones = pool.run(lambda: nc.gpsimd.memset(ones_f[:], 1.0))
    pool.run(lambda: nc.gpsimd.memset(eps_t[:], LN_EPS))
    t_onesb = pool.run(lambda: nc.gpsimd.memset(ones_bf[:], 1.0))
    t_idz = pool.run(lambda: nc.gpsimd.memset(ident_f[:], 0.0))
    t_identf = pool.run(lambda: nc.gpsimd.affine_select(
        out=ident_f[:], in_=ident_f[:], pattern=[[-1, 128]], compare_op=ALU.not_equal,
        fill=1.0, base=0, channel_multiplier=1), [t_idz])
    t_ident = dve.run(lambda: nc.vector.tensor_copy(ident_bf[:], ident_f[:]), [t_identf])
    t_id2a = dve.run(lambda: nc.vector.tensor_copy(ident2[0:64, :], ident_f[0:64, 0:64]), [t_identf])
    t_id2 = dve.run(lambda: nc.vector.tensor_copy(ident2[64:128, :], ident_f[64:128, 64:128]), [t_identf])
    t_esink = act.run(lambda: nc.scalar.activation(esink8[:], esink8[:], AF.Exp), [t_small])

    xT = V(R1, 0, [128, 8, T], BF16)
    memT = V(R2, 48 * KB, [128, 8, NMEM], BF16)
    NXS = 6
    xs = [V(R3, i * 2 * KB, [128, D], BF16) for i in range(NXS)]
    xs_slot = [new_slot("xs") for _ in range(NXS)]
    xs_free = [[] for _ in range(NXS)]
    blocks = [("x", tb) for tb in range(16)] + [("m", mb) for mb in range(2)]
    cw = V(R3, 16 * KB, [128, 128, 64], BF16)
    t_dg_last = {}
    wa = V(R5, 0, [128, 8, 512], BF16)
    wg = V(R5, 8 * KB, [128, 8, 512], BF16)
    s_wa, s_wg = [new_slot("wa") for _ in range(4)], [new_slot("wg") for _ in range(4)]
    UW = T + 32
    U2E = V(R4, 0, [128, 4, UW], BF16)
    U2O = V(R2, 16 * KB, [128, 4, UW], BF16)
    t_u_tok = {}
    sig = [V(R5, 16 * KB + i * 4 * KB, [128, 1024], F32) for i in range(2)]
    sig_free = [[], []]
    t_xT_ev = {}

    def p0_block(i):
        kind, bi = blocks[i]
        r = i % NXS
        src = x_d[bi * 128:(bi + 1) * 128, :] if kind == "x" else mem_d[bi * 128:(bi + 1) * 128, :]
        t_ld = dma(pool, xs[r], src, xs_slot[r], xs_free[r])
        b, bdeps = banks.alloc(1)
        for kc in range(8):
            t_tr = pe.run(lambda kc=kc: nc.tensor.transpose(
                psb[:, b, kc * 128:(kc + 1) * 128], xs[r][:, kc * 128:(kc + 1) * 128], ident_bf[:]),
                [t_ld, t_ident, bdeps] if kc == 0 else (), inc=(kc == 7))
        xs_free[r] = [t_tr]
        dst = xT[:, :, bi * 128:(bi + 1) * 128] if kind == "x" else memT[:, :, bi * 128:(bi + 1) * 128]
        srcp = psb[:, b, :].rearrange("p (a b) -> p a b", a=8)
        if i % 2 == 0:
            t_ev = act.run(lambda: nc.scalar.copy(dst, srcp), [t_tr])
        else:
            t_ev = dve.run(lambda: nc.vector.tensor_copy(dst, srcp), [t_tr])
        banks.release(b, t_ev)
        t_xT_ev[i] = t_ev
        for cj in range(i * 8, min(128, i * 8 + 8)):
            if i % 2 == 0:
                t_dg_last["dve"] = dve.run(lambda cj=cj: nc.vector.tensor_scalar(
                    out=cw[:, cj, :], in0=ident2[:], scalar1=dwsc[:, cj:cj + 1], scalar2=None, op0=ALU.mult),
                    [t_small, t_id2a, t_id2])
            else:
                t_dg_last["act"] = act.run(lambda cj=cj: nc.scalar.activation(
                    cw[:, cj, :], ident2[:], AF.Copy, scale=dwsc[:, cj:cj + 1]), [t_small, t_id2a, t_id2])

    a1_i = [0]
    sig4 = [V(R5, 16 * KB + i * 2 * KB, [128, 512], F32) for i in range(4)]
    sig4_free = [[], [], [], []]

    def a1_unit(c, q):
        xdeps = [t_xT_ev[i] for i in range(q * 4, q * 4 + 4)]
        tsl = slice(q * 512, (q + 1) * 512)
        bA, dA = banks.alloc(1)
        bG, dG = banks.alloc(1)
        for kc in range(8):
            tA = mm(ps[:, bA, :], wa[:, kc, c * 128:(c + 1) * 128], xT[:, kc, tsl], kc == 0, kc == 7,
                    [dA, t_wa[c], xdeps] if kc == 0 else (), inc=(kc == 7))
        for kc in range(8):
            tG = mm(ps[:, bG, :], wg[:, kc, c * 128:(c + 1) * 128], xT[:, kc, tsl], kc == 0, kc == 7,
                    [dG, t_wg[c]] if kc == 0 else (), inc=(kc == 7))
        r = a1_i[0] % 4
        a1_i[0] += 1
        t_sig = act.run(lambda: nc.scalar.activation(sig4[r], ps[:, bG, :], AF.Sigmoid), [tG, sig4_free[r]])
        banks.release(bG, t_sig)
        osl = slice(15 + q * 512, 15 + (q + 1) * 512)
        t_u0 = dve.run(lambda: nc.vector.tensor_tensor(
            out=U2E[0:64, c, osl], in0=ps[0:64, bA, :], in1=sig4[r][0:64, :], op=ALU.mult), [tA, t_sig])
        t_u = dve.run(lambda: nc.vector.tensor_tensor(
            out=U2O[64:128, c, osl], in0=ps[64:128, bA, :], in1=sig4[r][64:128, :], op=ALU.mult), [tA, t_sig])
        banks.release(bA, t_u)
        sig4_free[r] = [t_u]
        t_u_tok[c, q] = t_u

    t_wa, t_wg = [None] * 4, [None] * 4

    def load_glu_w(c):
        t_wa[c] = dma(pool, wa[:, :, c * 128:(c + 1) * 128], w_in_v[:, :, c * 128:(c + 1) * 128], s_wa[c])
        t_wg[c] = dma(pool, wg[:, :, c * 128:(c + 1) * 128], w_in_v[:, :, 512 + c * 128:512 + (c + 1) * 128], s_wg[c])

    for i in range(5):
        p0_block(i)
        if i == 1:
            load_glu_w(0)
        if i == 3:
            load_glu_w(1)
    t_pads = [pool.run(lambda: nc.gpsimd.memset(U2E[0:64, :, 0:15], 0.0)),
              pool.run(lambda: nc.gpsimd.memset(U2E[0:64, :, T + 15:UW], 0.0)),
              pool.run(lambda: nc.gpsimd.memset(U2O[64:128, :, 0:15], 0.0)),
              pool.run(lambda: nc.gpsimd.memset(U2O[64:128, :, T + 15:UW], 0.0))]
    rest = list(range(5, 18))
    s_shift = [new_slot("shift") for _ in range(4)]
    t_shift = [None] * 4
    for q in range(4):
        for c in range(4):
            a1_unit(c, q)
            if rest:
                p0_block(rest.pop(0))
            if q == 0 and c < 2:
                load_glu_w(c + 2)
            if q == 3:
                dma(sp, U2E[64:128, c, 0:UW - 1], U2E[0:64, c, 1:UW], s_shift[c],
                    [[t_u_tok[c, qq] for qq in range(4)], t_pads])
                t_shift[c] = dma(sp, U2O[0:64, c, 0:UW - 1], U2O[64:128, c, 1:UW], s_shift[c])
    assert not rest
    t_dg = [t_dg_last["dve"], t_dg_last["act"]]
    if stop_after == "a1":
        barrier()
        finish(nc, locals())
        return result()

    sT = V(R3, 0, [128, 4, T], BF16)
    cvb = [V(R5, i * 2 * KB, [128, 512], F32) for i in range(4)]
    sqb = [V(R5, 8 * KB + i * KB, [128, 512], BF16) for i in range(2)]
    cvh = [V(R5, 10 * KB + i * KB, [128, 512], BF16) for i in range(2)]
    meanb = V(R5, 12 * KB, [128, 512], F32)
    varb = V(R5, 14 * KB, [128, 512], F32)
    rstdb = V(R5, 16 * KB, [128, 512], F32)
    zb = [V(R5, 18 * KB + i * 2 * KB, [128, 512], F32) for i in range(3)] + [V(R3, 40 * KB, [128, 512], F32)]
    wq = V(R4, 16640, [128, 8, 512], BF16)
    wkv = V(R4, 16640 + 8 * KB, [128, 8, 384], BF16)
    s_wq, s_wkv = new_slot("wq"), new_slot("wkv")
    t_wq = dma(pool, wq, w_in_v[:, :, 1024:1536], s_wq)
    for g in range(2):
        for dup in range(2):
            dma(pool, wkv[:, :, g * 128 + dup * 64: g * 128 + (dup + 1) * 64],
                w_in_v[:, :, 1536 + g * 64: 1536 + (g + 1) * 64], s_wkv)
    t_wkv = dma(pool, wkv[:, :, 256:384], w_in_v[:, :, 1664:1792], s_wkv)

    s_w1c = new_slot("w1c")
    for kc in range(8):
        t_w1c = dma(pool, w1b_d[kc * 128:(kc + 1) * 128, :], w_f1_d[kc * 128:(kc + 1) * 128, :], s_w1c,
                    [t_shift] if kc == 0 else ())
    cvb2 = [cvb, [V(R3, 32 * KB + i * 2 * KB, [128, 512], F32) for i in range(4)]]
    cv_free = [[[] for _ in range(4)] for _ in range(2)]
    sq_free = [[], []]
    z_free = [[], [], [], []]
    cst = {"zi": 0, "sqi": 0, "stat_free": []}

    def cv_A(tt):
        cvs = cvb2[tt % 2]
        bS, dS = banks.alloc(1)
        bQ, dQ = banks.alloc(1)
        t_cv = [None] * 4
        pend = None
        for c in range(4):
            bC, dC = banks.alloc(1)
            for jg in range(16):
                win = slice(tt * 512 + 2 * jg, tt * 512 + 2 * jg + 512)
                mm(ps[0:64, bC, :], cw[:, c * 16 + jg, :], U2E[:, c, win],
                   jg == 0, jg == 15, [dC, t_dg, t_shift[c]] if jg == 0 else ())
                tk = mm(ps[64:128, bC, :], cw[:, 64 + c * 16 + jg, :], U2O[:, c, win],
                        jg == 0, jg == 15, (), inc=(jg == 15))
            t_cv[c] = act.run(lambda c=c: nc.scalar.activation(
                cvs[c], ps[:, bC, :], AF.Identity, bias=cvec[:, c:c + 1]), [tk, cv_free[tt % 2][c]])
            r = cst["sqi"] % 2
            cst["sqi"] += 1
            t_ch = act.run(lambda c=c: nc.scalar.activation(
                cvh[r], ps[:, bC, :], AF.Identity, bias=cvec[:, c:c + 1]), [sq_free[r]])
            t_sq = act.run(lambda c=c: nc.scalar.activation(
                sqb[r], ps[:, bC, :], AF.Square, bias=cvec[:, c:c + 1]), [sq_free[r]])
            banks.release(bC, t_sq)

            def stats_mm(c=c, r=r, t_sq=t_sq, t_ch=t_ch):
                mm(ps[:, bS, :], ones_bf[:], cvh[r], c == 0, c == 3, [t_ch, dS, t_onesb] if c == 0 else [t_ch])
                tq_ = mm(ps[:, bQ, :], ones_bf[:], sqb[r], c == 0, c == 3, [t_sq, dQ] if c == 0 else [t_sq], inc=True)
                sq_free[r] = [tq_]
                return tq_
            if pend is not None:
                pend()
            pend = stats_mm
        tq = pend()
        return (bS, bQ, tq, t_cv)

    def cv_B(tt, st):
        bS, bQ, tq, t_cv = st
        t_mean = dve.run(lambda: nc.vector.tensor_scalar(
            out=meanb, in0=ps[:, bS, :], scalar1=1.0 / 512, scalar2=None, op0=ALU.mult), [tq, cst["stat_free"]])
        t_msq = dve.run(lambda: nc.vector.tensor_tensor(out=varb, in0=meanb, in1=meanb, op=ALU.mult), [t_mean])
        t_var = dve.run(lambda: nc.vector.scalar_tensor_tensor(
            out=varb, in0=ps[:, bQ, :], scalar=1.0 / 512, in1=varb, op0=ALU.mult, op1=ALU.subtract), [t_msq])
        t_sd = act.run(lambda: nc.scalar.activation(rstdb, varb, AF.Sqrt, bias=eps_t[:, 0:1]), [t_var])
        t_rstd = dve.run(lambda: nc.vector.reciprocal(rstdb, rstdb), [t_sd])
        banks.release(bS, t_mean)
        banks.release(bQ, t_var)
        cst["stat_free"] = []
        return (t_mean, t_rstd, t_cv)

    def cv_C(tt, st):
        t_mean, t_rstd, t_cv = st
        cvs = cvb2[tt % 2]
        for c in range(4):
            r = cst["zi"] % 4
            cst["zi"] += 1
            t_z0 = dve.run(lambda c=c: nc.vector.tensor_tensor(out=zb[r], in0=cvs[c], in1=meanb, op=ALU.subtract),
                           [t_cv[c], t_mean, z_free[r]])
            t_z1 = dve.run(lambda: nc.vector.tensor_tensor(out=zb[r], in0=zb[r], in1=rstdb, op=ALU.mult),
                           [t_z0, t_rstd])
            t_s = act.run(lambda c=c: nc.scalar.activation(
                sT[:, c, tt * 512:(tt + 1) * 512], zb[r], AF.Silu,
                bias=cvec[:, 8 + c:9 + c], scale=cvec[:, 4 + c:5 + c]), [t_z1])
            z_free[r] = [t_s]
            cv_free[tt % 2][c] = [t_z0]
            cst["stat_free"].append(t_z1)

    stB = None
    for tt in range(5):
        stA = cv_A(tt) if tt < 4 else None
        if tt >= 1:
            cv_C(tt - 1, stB)
        if tt < 4:
            stB = cv_B(tt, stA)
    qT = V(R2, 0, [128, 4, T], BF16)
    evi = 0

    def proj_fm(wsl, col0, tw, dst):
        nonlocal evi
        for th in range(2):
            bb, dd = banks.alloc(2)
            for half in range(2):
                for kc in range(8):
                    tk = mm(ps[:, bb + half, :], wsl[:, kc, col0:col0 + 128],
                            xT[:, kc, th * 1024 + half * 512: th * 1024 + (half + 1) * 512],
                            kc == 0, kc == 7, [dd, tw] if (half == 0 and kc == 0) else (),
                            inc=(half == 1 and kc == 7))
            o = dst[:, th * 1024:(th + 1) * 1024]
            if evi % 2 == 0:
                te = act.run(lambda: nc.scalar.copy(o, bank2(bb)), [tk])
            else:
                te = dve.run(lambda: nc.vector.tensor_copy(o, bank2(bb)), [tk])
            evi += 1
            banks.release(bb, te, 2)
        return te

    for c in range(4):
        proj_fm(wq, c * 128, t_wq, qT[:, c, :])
    wqm = V(R4, 16640, [128, 8, 512], BF16)
    s_wqm = new_slot("wqm")
    t_wqm = dma(pool, wqm, w_in_v[:, :, 1792:2304], s_wqm, [pe.tok()])
    barrier()
    if stop_after == "b2":
        finish(nc, locals())
        return result()

    vaug = V(R2, 24 * KB, [128, 16, 2, 128], BF16)
    kTz = V(R4, 0, [128, 2, 2, T], BF16)
    va_flat = vaug.rearrange("p t g c -> p t (g c)")
    t_vones = pool.run(lambda: nc.gpsimd.memset(va_flat[:, :, 64:192], 1.0))
    t_kz0 = dve.run(lambda: nc.vector.memset(kTz[64:128, 0, :, :], 0.0))
    t_kz1 = dve.run(lambda: nc.vector.memset(kTz[0:64, 1, :, :], 0.0))
    wmk = [V(R3, 16 * KB + i * 8 * KB, [128, 8, 512], BF16) for i in range(2)]
    w_mkv_v = w_mkv_d.rearrange("(kc p) n -> p kc n", p=128)
    s_wmk = [new_slot("wmk") for _ in range(2)]
    t_wmk = [dma(pool, wmk[i], w_mkv_v[:, :, i * 512:(i + 1) * 512], s_wmk[i]) for i in range(2)]


    qmT = V(R2, 32 * KB, [128, 4, T], BF16)
    for g in range(2):
        for th in range(2):
            bb, dd = banks.alloc(2)
            for half in range(2):
                for kc in range(8):
                    tk = mm(ps[:, bb + half, :], wkv[:, kc, g * 128:(g + 1) * 128],
                            xT[:, kc, th * 1024 + half * 512: th * 1024 + (half + 1) * 512],
                            kc == 0, kc == 7, [dd, t_wkv] if (half == 0 and kc == 0) else (),
                            inc=(half == 1 and kc == 7))
            te0 = act.run(lambda: nc.scalar.copy(kTz[0:64, 0, g, th * 1024:(th + 1) * 1024], bank2(bb)[0:64, :]), [tk, t_kz0, t_kz1])
            te1 = dve.run(lambda: nc.vector.tensor_copy(kTz[64:128, 1, g, th * 1024:(th + 1) * 1024], bank2(bb)[64:128, :]), [tk, t_kz0, t_kz1])
            banks.release(bb, te0, 2)
            banks.release(bb, te1, 2)
    for tg in range(4):
        bb, dd = banks.alloc(1)
        for i4 in range(4):
            tb = tg * 4 + i4
            for kc in range(8):
                tk = mm(ps[:, bb, i4 * 128:(i4 + 1) * 128], xT[:, kc, tb * 128:(tb + 1) * 128],
                        wkv[:, kc, 256:384], kc == 0, kc == 7,
                        [dd, t_wkv] if (i4 == 0 and kc == 0) else (), inc=(i4 == 3 and kc == 7))
        o = bass.AP(vaug.tensor, vaug[:, tg * 4, 0, 0:1].offset,
                    [[vaug.ap[0][0], 128], [256, 4], [192, 2], [1, 64]])
        i_ = ps[:, bb, :].rearrange("p (t g c) -> p t g c", t=4, g=2)
        te = dve.run(lambda: nc.vector.tensor_copy(o, i_), [tk, t_vones])
        banks.release(bb, te)
    Btab = V(R5, 12 * KB, [128, 2, 3, 8, 128], BF16)
    Brev = V(R3, 32 * KB, [128, 2, 3, 8, 128], BF16)
    oh = V(R5, 0, [32, 512], F32)
    rv = V(R5, 2 * KB, [8, 512], F32)
    ge = V(R5, 4 * KB, [8, 512], F32)
    gneg = V(R5, 6 * KB, [8, 512], F32)
    hl = V(R5, 8 * KB, [8, 2, 512], F32)
    ghb = V(R5, 0, [8, 512], BF16)
    s_oh = new_slot("oh")
    dma(sp, oh, onehot_d, s_oh)
    t_oh = dma(sp, rv, relvalid_d.partition_broadcast(8).rearrange("p a b -> p (a b)"), s_oh)
    bM, dM = banks.alloc(1)
    t_g = mm(ps[0:8, bM, :], rb33[:], oh, True, True, [t_oh, t_small, dM], inc=True)
    t_b0 = dve.run(lambda: nc.vector.tensor_scalar(out=ge, in0=ps[0:8, bM, :], scalar1=8.0, scalar2=None, op0=ALU.mult), [t_g])
    banks.release(bM, t_b0)
    t_b1 = dve.run(lambda: nc.vector.tensor_tensor(out=ge, in0=ge, in1=rv, op=ALU.mult), [t_b0])
    t_b2 = dve.run(lambda: nc.vector.tensor_scalar(out=gneg, in0=rv, scalar1=800.0, scalar2=-800.0,
                                                   op0=ALU.mult, op1=ALU.add), [t_oh])
    t_b3 = dve.run(lambda: nc.vector.tensor_tensor(out=ge, in0=ge, in1=gneg, op=ALU.add), [t_b1, t_b2])
    t_b4 = dve.run(lambda: nc.vector.tensor_copy(ghb, ge), [t_b3, t_g])
    t_b5 = dve.run(lambda: nc.vector.tensor_copy(hl[:, 0, :], ghb), [t_b4])
    t_b6 = dve.run(lambda: nc.vector.tensor_tensor(out=hl[:, 1, :], in0=ge, in1=hl[:, 0, :], op=ALU.subtract), [t_b5])
    s_ge = new_slot("ge")
    t_gst = dma(sp, gexp_d.ap().rearrange("a p n -> p a n"), hl, s_ge, [t_b6])
    s_mt = new_slot("mt")
    for t in range(2):
        for kb in range(3):
            for h2 in range(2):
                src = bass.AP(gexp_d, h2 * 4096 + kb * 128 + t * 64, [[1, 64], [512, 8], [1, 128]])
                t_brev = dma(pool, Brev[h2 * 64:(h2 + 1) * 64, t, kb, :, :], src, s_mt, [t_gst])

    for c in range(4):
        proj_fm(wqm, c * 128, t_wqm, qmT[:, c, :])
    act.run(lambda: nc.scalar.activation(stt[:, 3, 2:3], eps_t[:, 0:1], AF.Exp))
    barrier()
    if stop_after == "a2":
        finish(nc, locals())
        return result()

    wgs0 = V(R5, 0, [128, 3, 8, 256], BF16)
    s_wg0 = [new_slot("wg0") for _ in range(3)]
    t_wg0 = [dma(pool, wgs0[:, b, :, :], w_in_v[:, :, 2304 + b * 1024: 2304 + b * 1024 + 256], s_wg0[b]) for b in range(3)]

    omT = V(R3, 32 * KB, [128, 4, T], BF16)
    kmT = V(R2, 52 * KB, [128, 4, NMEM], BF16)
    vm129 = V(R2, 54 * KB, [128, 2, 4, 129], BF16)
    PTm = [V(R2, 57 * KB + i * 2 * KB, [128, 2, 512], BF16) for i in range(2)]
    omtm = [V(R2, 61 * KB + i * KB, [128, 4, 128], BF16) for i in range(2)]
    t_vm1 = pool.run(lambda: nc.gpsimd.memset(vm129[:, :, :, 128:129], 1.0))
    for hp in range(2):
        bb, dd = banks.alloc(1)
        for hh in range(2):
            h = hp * 2 + hh
            for kc in range(8):
                tk = mm(ps[:, bb, hh * 256:(hh + 1) * 256], wmk[0][:, kc, h * 128:(h + 1) * 128], memT[:, kc, :],
                        kc == 0, kc == 7, [dd, t_wmk[0]] if (hh == 0 and kc == 0) else (), inc=(hh == 1 and kc == 7))
        te = dve.run(lambda: nc.vector.tensor_copy(
            kmT[:, hp * 2:hp * 2 + 2, :], ps[:, bb, :].rearrange("p (a b) -> p a b", a=2)), [tk])
        banks.release(bb, te)
    t_kmT = te
    for mc in range(2):
        bb, dd = banks.alloc(1)
        for kc in range(8):
            tk = mm(ps[:, bb, :], memT[:, kc, mc * 128:(mc + 1) * 128], wmk[1][:, kc, :],
                    kc == 0, kc == 7, [dd, t_wmk[1]] if kc == 0 else (), inc=(kc == 7))
        te = dve.run(lambda: nc.vector.tensor_copy(
            vm129[:, mc, :, 0:128], ps[:, bb, :].rearrange("p (a b) -> p a b", a=4)), [tk])
        banks.release(bb, te)
    t_vm = [te, t_vm1]
    t_btab = None
    for h2 in range(2):
        for kb in range(3):
            src = bass.AP(Brev.tensor, Brev[:, h2, kb, 0, 127:128].offset, [[Brev.ap[0][0], 128], [128, 8], [-1, 128]])
            t_btab = dve.run(lambda: nc.vector.tensor_copy(Btab[:, h2, kb, :, :], src), [t_brev])
    PTm_free = [[], []]
    om_free = [[], []]
    m_state = {}
    scale_m = 1.0 / math.sqrt(128.0)
    munits = [(tt, h) for tt in range(4) for h in range(4)]

    def m_S(u):
        tt, h = munits[u]
        r = u % 2
        bS, dS = banks.alloc(2)
        for mc in range(2):
            tk = mm(ps[:, bS + mc, :], kmT[:, h, mc * 128:(mc + 1) * 128], qmT[:, h, tt * 512:(tt + 1) * 512],
                    True, True, [dS, t_kmT] if mc == 0 else (), inc=(mc == 1))
        t_e = act.run(lambda: nc.scalar.activation(
            PTm[r].rearrange("p a b -> p (a b)"), bank2(bS), AF.Exp, scale=scale_m), [tk, PTm_free[r]])
        banks.release(bS, t_e, 2)
        m_state["S", u] = t_e

    def m_V(u):
        tt, h = munits[u]
        r = u % 2
        t_e = m_state.pop(("S", u))
        bV, dV = banks.alloc(2)
        for tbl in range(4):
            for mc in range(2):
                tk = mm(ps[:, bV + tbl // 2, (tbl % 2) * 256:(tbl % 2) * 256 + 129],
                        PTm[r][:, mc, tbl * 128:(tbl + 1) * 128], vm129[:, mc, h, :], mc == 0, mc == 1,
                        [t_e, dV, t_vm] if (tbl == 0 and mc == 0) else (), inc=(tbl == 3 and mc == 1))
        PTm_free[r] = [tk]
        dn = dnt[:, u % 2, :]
        pst = ps[:, bV:bV + 2, :].tensor
        base = ps[:, bV, 0:1].offset
        pstr = ps[:].ap[0][0]
        den = bass.AP(pst, base + 128, [[pstr, 128], [512, 2], [256, 2]])
        num = bass.AP(pst, base, [[pstr, 128], [512, 2], [256, 2], [1, 128]])
        t_rc = dve.run(lambda: nc.vector.reciprocal(dn[:, 0:4].rearrange("p (a b) -> p a b", a=2), den), [tk])
        rd_b = bass.AP(dnt, dn[:, 0:1].offset, [[dnt[:].ap[0][0], 128], [2, 2], [1, 2], [0, 128]])
        t_o = dve.run(lambda: nc.vector.tensor_tensor(
            out=omtm[r].rearrange("p (a b) c -> p a b c", a=2), in0=num, in1=rd_b, op=ALU.mult), [t_rc, om_free[r]])
        banks.release(bV, t_o, 2)
        m_state["V", u] = t_o

    def m_T(u):
        tt, h = munits[u]
        r = u % 2
        t_o = m_state.pop(("V", u))
        bT, dT = banks.alloc(1)
        for tbl in range(4):
            t_tr = pe.run(lambda: nc.tensor.transpose(
                psb[:, bT, tbl * 128:(tbl + 1) * 128], omtm[r][:, tbl, :], ident_bf[:]),
                [t_o, dT] if tbl == 0 else (), inc=(tbl == 3))
        om_free[r] = [t_tr]
        if u % 2 == 0:
            t_ev = dve.run(lambda: nc.vector.tensor_copy(omT[:, h, tt * 512:(tt + 1) * 512], psb[:, bT, 0:512]), [t_tr, t_btab])
        else:
            t_ev = act.run(lambda: nc.scalar.copy(omT[:, h, tt * 512:(tt + 1) * 512], psb[:, bT, 0:512]), [t_tr, t_btab])
        banks.release(bT, t_ev)

    NM = len(munits)
    m_S(0)
    for u in range(NM + 1):
        if u + 1 < NM:
            m_S(u + 1)
        if u < NM:
            m_V(u)
        if u >= 1:
            m_T(u - 1)
    barrier()
    if stop_after == "b1":
        finish(nc, locals())
        return result()

    wos0 = V(R2, 56 * KB, [128, 3, 4, 256], BF16)
    s_wos0 = [new_slot("wos0") for _ in range(3)]
    w_out_v0 = [w_co_d.rearrange("(kc p) n -> p kc n", p=128),
                w_ao_d.rearrange("(kc p) n -> p kc n", p=128),
                w_mo_d.rearrange("(kc p) n -> p kc n", p=128)]
    t_wos0 = [dma(pool, wos0[:, b, :, :], w_out_v0[b][:, :, 0:256], s_wos0[b]) for b in range(3)]

    oT = V(R3, 16 * KB, [128, 4, T], BF16)
    otm = [V(R4, 16 * KB + i * 512, [128, 4, 64], BF16) for i in range(3)]
    PTw = [V(R4, 22 * KB + i * KB, [128, 512], BF16) for i in range(6)]
    otm_free = [[] for _ in range(3)]
    PT_free = [[] for _ in range(6)]
    w_state = {}
    pcount = [0]
    units = [(n, g) for n in range(16) for g in range(2)]

    def w_S(u):
        n, g = units[u]
        kbs = [kb for kb in range(3) if 0 <= n + kb - 1 < 16]
        res = []
        for kb in kbs:
            kblk = n + kb - 1
            bS, dS = banks.alloc(1)
            for half in range(2):
                mm(ps[:, bS, half * 256:(half + 1) * 256].rearrange("p (a b) -> p a b", a=2),
                   kTz[:, half, g, kblk * 128:(kblk + 1) * 128],
                   qT[:, 2 * g:2 * g + 2, n * 128:(n + 1) * 128],
                   half == 0, False, [dS] if half == 0 else ())
            for t in range(2):
                tk = mm(ps[t * 64:(t + 1) * 64, bS, :], ident2[:],
                        Btab[:, t, kb, g * 4:(g + 1) * 4, :].rearrange("p a b -> p (a b)"),
                        False, True, [t_btab, t_id2a, t_id2] if t == 0 else (), inc=(t == 1))
            rp = pcount[0] % 6
            pcount[0] += 1
            t_p = act.run(lambda: nc.scalar.activation(PTw[rp], ps[:, bS, :], AF.Exp, scale=0.125), [tk, PT_free[rp]])
            banks.release(bS, t_p)
            res.append((kb, rp, t_p))
        w_state["S", u] = res

    def w_V(u):
        n, g = units[u]
        res = w_state.pop(("S", u))
        bO, dO = banks.alloc(1)
        c0 = 0 if g == 0 else 63
        for s4 in range(4):
            for i, (kb, rp, t_p) in enumerate(res):
                kblk = n + kb - 1
                tk = mm(ps[:, bO, s4 * 128: s4 * 128 + 65], PTw[rp][:, s4 * 128:(s4 + 1) * 128],
                        vaug[:, kblk, g, c0:c0 + 65], i == 0, i == len(res) - 1,
                        [t_p, dO] if s4 == 0 else (), inc=(s4 == 3 and i == len(res) - 1))
        for (kb, rp, t_p) in res:
            PT_free[rp] = [tk]
        ncol, dcol = (0, 64) if g == 0 else (1, 0)
        r3 = u % 3
        dn = dnt[:, u % 2, :]
        pv = ps[:, bO, :].rearrange("p (a b) -> p a b", a=4)
        t_dn = dve.run(lambda: nc.vector.tensor_tensor(
            out=dn[:, 0:4], in0=pv[:, :, dcol], in1=esink8[:, g * 4:(g + 1) * 4], op=ALU.add), [tk, t_esink])
        t_rc = dve.run(lambda: nc.vector.reciprocal(dn[:, 4:8], dn[:, 0:4]), [t_dn])
        rd_b = bass.AP(dnt, dn[:, 4:5].offset, [[dnt[:].ap[0][0], 128], [1, 4], [0, 64]])
        t_o = dve.run(lambda: nc.vector.tensor_tensor(
            out=otm[r3], in0=pv[:, :, ncol:ncol + 64], in1=rd_b, op=ALU.mult), [t_rc, otm_free[r3]])
        banks.release(bO, t_o)
        w_state["V", u] = (r3, t_o)

    def w_T(u):
        n, g = units[u]
        r3, t_o = w_state.pop(("V", u))
        if g == 0:
            w_state["bT", n] = banks.alloc(1)
        bT, dT = w_state["bT", n]
        of = otm[r3].rearrange("p a b -> p (a b)")
        for sp2 in range(2):
            c4 = g * 2 + sp2
            t_tr = pe.run(lambda: nc.tensor.transpose(
                psb[:, bT, c4 * 128:(c4 + 1) * 128], of[:, sp2 * 128:(sp2 + 1) * 128], ident_bf[:]),
                [t_o, dT] if sp2 == 0 else (), inc=(sp2 == 1))
        otm_free[r3] = [t_tr]
        if g == 1:
            src = psb[:, bT, 0:512].rearrange("p (a b) -> p a b", a=4)
            t_ev = dve.run(lambda: nc.vector.tensor_copy(oT[:, :, n * 128:(n + 1) * 128], src), [t_tr])
            banks.release(bT, t_ev)
            del w_state["bT", n]

    NU = len(units)
    w_S(0)
    for u in range(NU + 1):
        if u + 1 < NU:
            w_S(u + 1)
        if u < NU:
            w_V(u)
        if u >= 1:
            w_T(u - 1)
    act.run(lambda: nc.scalar.activation(stt[:, 3, 3:4], eps_t[:, 0:1], AF.Sigmoid))
    barrier()
    if stop_after == "b3":
        finish(nc, locals())
        return result()

    merged = V(R4, 0, [128, 8, T], BF16)
    wgs = [V(R2, i * 18 * KB, [128, 3, 8, 256], BF16) for i in range(2)]
    wos = [V(R2, i * 18 * KB + 12 * KB, [128, 3, 4, 256], BF16) for i in range(2)]
    sgb = [V(R2, 36 * KB + i * 2 * KB, [128, 512], F32) for i in range(6)]
    mb = [V(R2, 48 * KB + i * 2 * KB, [128, 512], F32) for i in range(4)]
    s_wc = [[new_slot("wc") for _ in range(6)] for _ in range(2)]
    wc_free = [[], []]
    w_out_v = [w_co_d.rearrange("(kc p) n -> p kc n", p=128),
               w_ao_d.rearrange("(kc p) n -> p kc n", p=128),
               w_mo_d.rearrange("(kc p) n -> p kc n", p=128)]

    def load_wc(dp):
        r = dp % 2
        toks = {}
        for b in range(3):
            if dp == 0:
                toks["g", b] = t_wg0[b]
            else:
                toks["g", b] = dma(pool, wgs[r][:, b, :, :],
                                   w_in_v[:, :, 2304 + b * 1024 + dp * 256: 2304 + b * 1024 + (dp + 1) * 256],
                                   s_wc[r][b], wc_free[r] if b == 0 else ())
            if dp == 0:
                toks["o", b] = t_wos0[b]
            else:
                toks["o", b] = dma(pool, wos[r][:, b, :, :], w_out_v[b][:, :, dp * 256:(dp + 1) * 256], s_wc[r][3 + b],
                                   wc_free[r] if b == 0 else ())
        return toks

    srcs = None
    sg_free = [[] for _ in range(6)]
    m_free = [[] for _ in range(4)]
    sgi = mi = 0
    t_wc = {0: load_wc(0)}
    w_o_sb = V(R5, 0, [128, 8, D], BF16)
    ln1t = V(R5, 16 * KB, [128, 2, D], F32)
    s_ln1 = new_slot("ln1")
    t_ln1 = dma(sp, ln1t, bass.AP(ln1_d.tensor, 0, [[0, 128], [D, 2], [1, D]]), s_ln1)
    s_wo = new_slot("wo")
    for dp in range(4):
        if dp + 1 < 4:
            t_wc[dp + 1] = load_wc(dp + 1)
        if dp == 1:
            t_wo = dma(pool, w_o_sb, w_o_d.rearrange("(kc p) n -> p kc n", p=128), s_wo, [wg0_done])
        r = dp % 2
        wg_cur = wgs0 if dp == 0 else wgs[r]
        wo_cur = wos0 if dp == 0 else wos[r]
        last_pe = None
        for tt in range(4):
            tok_sl = slice(tt * 512, (tt + 1) * 512)
            for dci in range(2):
                dc = dp * 2 + dci
                cs = slice(dci * 128, (dci + 1) * 128)
                bra = [sT, oT, omT]
                t_sg = [None] * 3
                sg_r = [None] * 3
                bY = [None] * 3
                tY = [None] * 3
                for b in range(3):
                    bG, dG = banks.alloc(1)
                    for kc in range(8):
                        tk = mm(ps[:, bG, :], wg_cur[:, b, kc, cs], xT[:, kc, tok_sl], kc == 0, kc == 7,
                                [dG, t_wc[dp]["g", b]] if kc == 0 else (), inc=(kc == 7))
                    sr = sgi % 6
                    sgi += 1
                    t_sg[b] = act.run(lambda b=b: nc.scalar.activation(
                        sgb[sr], ps[:, bG, :], AF.Sigmoid, bias=bgate[:, b * 8 + dc: b * 8 + dc + 1]),
                        [tk, sg_free[sr]])
                    sg_r[b] = sr
                    banks.release(bG, t_sg[b])
                    bY[b], dY = banks.alloc(1)
                    for kc in range(4):
                        tY[b] = mm(ps[:, bY[b], :], wo_cur[:, b, kc, cs], bra[b][:, kc, tok_sl], kc == 0, kc == 3,
                                   [dY, t_wc[dp]["o", b]] if kc == 0 else (), inc=(kc == 3))
                    last_pe = tY[b]
                m0 = mi % 4
                m1 = (mi + 1) % 4
                mi += 2
                t0 = dve.run(lambda: nc.vector.tensor_tensor(out=mb[m0], in0=ps[:, bY[0], :], in1=sgb[sg_r[0]], op=ALU.mult),
                             [tY[0], t_sg[0], m_free[m0]])
                banks.release(bY[0], t0)
                t1 = dve.run(lambda: nc.vector.tensor_tensor(out=mb[m1], in0=ps[:, bY[1], :], in1=sgb[sg_r[1]], op=ALU.mult),
                             [tY[1], t_sg[1], m_free[m1]])
                banks.release(bY[1], t1)
                sg_free[sg_r[0]] = [t0]
                sg_free[sg_r[1]] = [t1]
                t2 = dve.run(lambda: nc.vector.tensor_tensor(out=mb[m0], in0=mb[m0], in1=mb[m1], op=ALU.add), [t0, t1])
                t3 = dve.run(lambda: nc.vector.tensor_tensor(out=mb[m1], in0=ps[:, bY[2], :], in1=sgb[sg_r[2]], op=ALU.mult),
                             [tY[2], t_sg[2], t2])
                banks.release(bY[2], t3)
                sg_free[sg_r[2]] = [t3]
                t4 = dve.run(lambda: nc.vector.tensor_tensor(out=merged[:, dc, tok_sl], in0=mb[m0], in1=mb[m1], op=ALU.add),
                             [t2, t3])
                m_free[m0] = [t4]
                m_free[m1] = [t4]
        wc_free[r] = [last_pe]
        if dp == 0:
            wg0_done = last_pe
    barrier()
    if stop_after == "c":
        finish(nc, locals())
        return result()

    x1 = V(R2, 0, [128, 16, D], F32)
    x1T = V(R1, 0, [128, 8, T], BF16)
    x1b = [V(R3, 44 * KB + i * 2 * KB, [128, D], BF16) for i in range(2)] + [x1b_extra[:]]
    w2 = V(R3, 0, [128, 22, D], BF16)
    s_w2 = new_slot("w2")
    w_f2_v = w_f2_d.rearrange("(j p) n -> p j n", p=128)
    s_x = [new_slot("xr") for _ in range(16)]
    t_x = [dma(sp, x1[:, tb, :], x_d[tb * 128:(tb + 1) * 128, :], s_x[tb]) for tb in range(16)]
    x1b_free = [[], [], []]

    def ln_stats(buf, deps, k, junk):
        st = stt[:, k % 4, :]
        tb_ = act.run(lambda: nc.scalar.activation(junk, buf, AF.Square, accum_out=st[:, 1:2]), deps)
        return tb_

    def ln_tiny(t_st, k, sum_ap):
        st = stt[:, k % 4, :]
        mv = mvt[:, k % 4, :]
        t1 = dve.run(lambda: nc.vector.tensor_scalar(
            out=mv[:, 0:1], in0=sum_ap, scalar1=1.0 / D, scalar2=None, op0=ALU.mult), [t_st])
        t2 = dve.run(lambda: nc.vector.tensor_tensor(out=mv[:, 1:2], in0=mv[:, 0:1], in1=mv[:, 0:1], op=ALU.mult), [t1])
        t3 = dve.run(lambda: nc.vector.scalar_tensor_tensor(
            out=mv[:, 1:2], in0=st[:, 1:2], scalar=1.0 / D, in1=mv[:, 1:2], op0=ALU.mult, op1=ALU.subtract), [t2])
        return t3

    def ln_sqrt(t3, k):
        mv = mvt[:, k % 4, :]
        return act.run(lambda: nc.scalar.activation(mv[:, 2:3], mv[:, 1:2], AF.Sqrt, bias=eps_t[:, 0:1]), [t3])

    def ln_recip(td0, k):
        mv = mvt[:, k % 4, :]
        return dve.run(lambda: nc.vector.reciprocal(mv[:, 2:3], mv[:, 2:3]), [td0])

    def ln_rstd(t_st, k, sum_ap):
        return ln_recip(ln_sqrt(ln_tiny(t_st, k, sum_ap), k), k)

    def ln_affine(buf, tabs, dst, t_rs, k, deps=()):
        mv = mvt[:, k % 4, :]
        te = dve.run(lambda: nc.vector.scalar_tensor_tensor(
            out=buf, in0=buf, scalar=mv[:, 0:1], in1=tabs[:, 0, :], op0=ALU.subtract, op1=ALU.mult), [t_rs, deps])
        tf = dve.run(lambda: nc.vector.scalar_tensor_tensor(
            out=dst, in0=buf, scalar=mv[:, 2:3], in1=tabs[:, 1, :], op0=ALU.mult, op1=ALU.add), [te])
        return tf

    def d_mm(tb):
        bO, dO = banks.alloc(2)
        for half in range(2):
            for kc in range(8):
                tk = mm(ps[:, bO + half, :], merged[:, kc, tb * 128:(tb + 1) * 128],
                        w_o_sb[:, kc, half * 512:(half + 1) * 512], kc == 0, kc == 7,
                        [dO, t_wo] if (half == 0 and kc == 0) else (), inc=(half == 1 and kc == 7))
        return bO, tk

    d_state = {}
    t_x1T = [None] * 16
    d_state_last_mm = [None]

    def d_H(tb):
        bO, tk = d_state.pop(("mm", tb))
        xb = x1[:, tb, :]
        t_h = dve.run(lambda: nc.vector.scalar_tensor_tensor(
            out=xb, in0=xb, scalar=ALPHA, in1=bank2(bO), op0=ALU.mult, op1=ALU.add,
            accum_out=hsum[:, 0, tb:tb + 1]), [tk, t_x[tb]])
        banks.release(bO, t_h, 2)
        d_state["h", tb] = t_h

    def d_S(tb):
        xb = x1[:, tb, :]
        r = tb % 3
        d_state["st", tb] = ln_stats(xb, [d_state.pop(("h", tb)), x1b_free[r]], tb, x1b[r])

    def d_tiny(tb):
        d_state["t3", tb] = ln_tiny(d_state.pop(("st", tb)), tb, hsum[:, 0, tb:tb + 1])

    def d_sqrt(tb):
        d_state["sq", tb] = ln_sqrt(d_state.pop(("t3", tb)), tb)

    def d_recip(tb):
        d_state["rs", tb] = ln_recip(d_state.pop(("sq", tb)), tb)

    def d_C(tb):
        xb = x1[:, tb, :]
        r = tb % 3
        t_ln = ln_affine(xb, ln1t, xb, d_state.pop(("rs", tb)), tb, [t_ln1])
        d_state["cb", tb] = act.run(lambda: nc.scalar.copy(x1b[r], xb), [t_ln])

    def d_Tpe(tb):
        r = tb % 3
        t_cb = d_state.pop(("cb", tb))
        bT, dT = banks.alloc(1)
        for kc in range(8):
            t_tr = pe.run(lambda kc=kc: nc.tensor.transpose(
                psb[:, bT, kc * 128:(kc + 1) * 128], x1b[r][:, kc * 128:(kc + 1) * 128], ident_bf[:]),
                [t_cb, dT] if kc == 0 else (), inc=(kc == 7))
        x1b_free[r] = [t_tr]
        d_state["tr", tb] = (bT, t_tr)

    def d_Tev(tb):
        bT, t_tr = d_state.pop(("tr", tb))
        t_ev = act.run(lambda: nc.scalar.copy(
            x1T[:, :, tb * 128:(tb + 1) * 128], psb[:, bT, :].rearrange("p (a b) -> p a b", a=8)),
            [t_tr, e_state.get("tmp_slab_done") if tb >= 12 else None])
        banks.release(bT, t_ev)
        t_x1T[tb] = t_ev

    actT = V(R4, 0, [128, 22, 512], BF16)
    ln2t = V(R4, 22 * KB, [128, 2, D], F32)
    sgf = [V(R4, 30 * KB, [128, 512], F32), sgf_extra[:]]
    w1s = [V(R5, i * 8 * KB, [128, 2, 8, 256], BF16) for i in range(2)]
    ost = [V(R5, 16 * KB + i * 4 * KB, [128, D], F32) for i in range(2)]
    s_ln2 = new_slot("ln2")
    w_f1_v = w1b_d.rearrange("(kc p) n -> p kc n", p=128)
    s_w1 = [new_slot("w1") for _ in range(2)]
    w1_free = [[], []]
    w1_i = [0]

    def load_w1(jp):
        r = w1_i[0] % 2
        w1_i[0] += 1
        dma(sp, w1s[r][:, 0, :, :], w_f1_v[:, :, jp * 256:(jp + 1) * 256], s_w1[r], [w1_free[r], t_w1c])
        return r, dma(sp, w1s[r][:, 1, :, :], w_f1_v[:, :, DFF + jp * 256: DFF + (jp + 1) * 256], s_w1[r])

    sg_free = [[], []]
    ost_free = [[], []]
    s_out = [new_slot("out") for _ in range(2)]
    seq = [(Q, jp) for Q in range(4) for jp in range(11)]
    e_state = {"sgi": 0, "oi": 0}

    w1s.append(bass.AP(x1T.tensor, x1T[:, 0, 1536:1537].offset, [[x1T.ap[0][0], 128], [256, 2], [T, 8], [1, 256]]))
    s_w1.append(new_slot("w1t"))
    dma(sp, w1s[2][:, 0, :, :], w_f1_v[:, :, 0:256], s_w1[2], [t_w1c])
    t_w1_first = dma(sp, w1s[2][:, 1, :, :], w_f1_v[:, :, DFF:DFF + 256], s_w1[2])

    def e_start(dep):
        e_state["t_ln2"] = dma(sp, ln2t, bass.AP(ln2_d.tensor, 0, [[0, 128], [D, 2], [1, D]]), s_ln2, [dep])
        w1_free[0] = [dep]
        w1_free[1] = [dep]
        e_state["pending"] = (2, t_w1_first)

    def e_step(si):
        Q, jp = seq[si]
        sgi = e_state["sgi"]
        oi = e_state["oi"]
        t_ln2 = e_state["t_ln2"]
        r, t_w1 = e_state["pending"]
        if si + 1 < len(seq):
            e_state["pending"] = load_w1(seq[si + 1][1])
        tok_sl = slice(Q * 512, (Q + 1) * 512)
        for jj in range(2):
            j = jp * 2 + jj
            cs = slice(jj * 128, (jj + 1) * 128)
            bG, dG = banks.alloc(1)
            for kc in range(8):
                tG = mm(ps[:, bG, :], w1s[r][:, 0, kc, cs], x1T[:, kc, tok_sl], kc == 0, kc == 7,
                        [dG, t_w1, t_x1T[4 * Q:4 * Q + 4]] if kc == 0 else (), inc=(kc == 7))
            bU, dU = banks.alloc(1)
            for kc in range(8):
                tU = mm(ps[:, bU, :], w1s[r][:, 1, kc, cs], x1T[:, kc, tok_sl], kc == 0, kc == 7,
                        [dU] if kc == 0 else (), inc=(kc == 7))
            sr = sgi % 2
            sgi += 1
            t_s = act.run(lambda: nc.scalar.activation(sgf[sr], ps[:, bG, :], AF.Silu), [tG, sg_free[sr]])
            banks.release(bG, t_s)
            t_a = dve.run(lambda: nc.vector.tensor_tensor(out=actT[:, j, :], in0=ps[:, bU, :], in1=sgf[sr], op=ALU.mult),
                          [tU, t_s])
            banks.release(bU, t_a)
            sg_free[sr] = [t_a]
        if r < 2:
            w1_free[r] = [tU]
        else:
            e_state["tmp_slab_done"] = tU
        if jp == 10:
            for tbl in range(4):
                tb = Q * 4 + tbl
                bO, dO = banks.alloc(2)
                for half in range(2):
                    for j in range(22):
                        tk = mm(ps[:, bO + half, :], actT[:, j, tbl * 128:(tbl + 1) * 128],
                                w2[:, j, half * 512:(half + 1) * 512], j == 0, j == 21,
                                [dO, t_w2, t_a] if (half == 0 and j == 0) else (), inc=(half == 1 and j == 21))
                xb = x1[:, tb, :]
                t_h = dve.run(lambda: nc.vector.scalar_tensor_tensor(
                    out=xb, in0=xb, scalar=ALPHA, in1=bank2(bO), op0=ALU.mult, op1=ALU.add,
                    accum_out=hsum[:, 1, tb:tb + 1]), [tk])
                banks.release(bO, t_h, 2)
                ro = oi % 2
                oi += 1
                t_st2 = ln_stats(xb, [t_h, ost_free[ro]], tb, ost[ro])
                t_rs2 = ln_rstd(t_st2, tb, hsum[:, 1, tb:tb + 1])
                t_ln = ln_affine(xb, ln2t, ost[ro], t_rs2, tb, [t_ln2])
                t_st = dma(pool, out_d[tb * 128:(tb + 1) * 128, :], ost[ro], s_out[ro], [t_ln])
                ost_free[ro] = [t_st]
        e_state["sgi"] = sgi
        e_state["oi"] = oi

    for tb in range(9):
        d_state["mm", tb] = d_mm(tb)
        d_state_last_mm[0] = d_state["mm", tb][1]
        if tb >= 1:
            d_H(tb - 1)
    for tb in range(2):
        d_S(tb)
        d_tiny(tb)
    d_sqrt(0)
    d_recip(0)
    for tb in range(17):
        if tb >= 1:
            d_Tpe(tb - 1)
        if tb + 9 < 16:
            d_state["mm", tb + 9] = d_mm(tb + 9)
            d_state_last_mm[0] = d_state["mm", tb + 9][1]
        if tb == 6:
            dma(pool, w2[:, 0:11, :], w_f2_v[:, 0:11, :], s_w2)
            t_w2 = dma(pool, w2[:, 11:22, :], w_f2_v[:, 11:22, :], s_w2)
        if tb + 1 < 16:
            d_sqrt(tb + 1)
        if tb >= 1:
            d_Tev(tb - 1)
        if tb + 2 < 16:
            d_S(tb + 2)
        if tb < 16:
            d_C(tb)
        if tb + 1 < 16:
            d_recip(tb + 1)
        if tb + 8 < 16:
            d_H(tb + 8)
        if tb + 2 < 16:
            d_tiny(tb + 2)
        if tb == 6:
            e_start(d_state_last_mm[0])
        if tb >= 7:
            e_step(tb - 7)
    n_e_done = 10
    if stop_after == "d":
        barrier()
        finish(nc, locals())
        return result()
    for si in range(n_e_done, len(seq)):
        e_step(si)
    pool.wait([s.tok() for s in s_out])
    return result()


def finish(nc, env):
    taps = env["taps"]
    sp = env["sp"]
    slot = env["new_slot"]("tap")
    for name, getter in taps.items():
        ap, shape, dt = getter(env)
        o = nc.dram_tensor("tap_" + name, list(shape), dt, kind="ExternalOutput").ap()
        sp.eng.dma_start(out=o, in_=ap).then_inc(slot.sem, 16)
        slot.count += 16
    sp.wait([slot.tok()])
    return nc


def _t5_bucket_np(rel):
    half, max_exact = 16, 8
    base = np.where(rel > 0, half, 0)
    n = np.abs(rel)
    nf = np.maximum(n, 1).astype(np.float32)
    large = max_exact + (np.log(nf / np.float32(max_exact)) / np.float32(math.log(128 / max_exact))
                         * np.float32(half - max_exact)).astype(np.int32)
    large = np.minimum(large, half - 1)
    return base + np.where(n < max_exact, n, large)


def _consts():
    j = np.arange(512)
    rel = j - 255
    valid = (np.abs(rel) <= 128) & (j < 511)
    bucket = _t5_bucket_np(rel.astype(np.int32))
    onehot = np.zeros((32, 512), np.float32)
    onehot[bucket[valid], j[valid]] = 1.0
    relvalid = valid.astype(np.float32)[None, :]
    return onehot, relvalid


def prepare_inputs(inputs):
    f = lambda a: np.ascontiguousarray(np.asarray(a, dtype=np.float32))
    onehot, relvalid = _consts()
    w_ao = np.asarray(inputs["w_attn_out"][0], np.float32)
    w_ao_p = np.concatenate(
        [w_ao[PERM[g * 4 + sp * 2 + s2] * 64:(PERM[g * 4 + sp * 2 + s2] + 1) * 64]
         for g in range(2) for sp in range(2) for s2 in range(2)], 0)
    dww = np.asarray(inputs["conv_dw_w"][0], np.float32)
    dwp = np.concatenate([dww, np.zeros((1, 512), np.float32)], 0).reshape(16, 2, 4, 2, 64)
    scE = dwp[:, :, :, 0, :].transpose(1, 3, 2, 0).reshape(128, 4, 16)
    scO = dwp[:, ::-1, :, 1, :].transpose(1, 3, 2, 0).reshape(128, 4, 16)
    dwsc = np.stack([scE, scO], 1).reshape(128, 128)
    cv = np.stack([np.asarray(inputs[k][0], np.float32).reshape(4, 128).T
                   for k in ("conv_dw_b", "conv_ln_g", "conv_ln_b")], 1).reshape(128, 12)
    shared = {
        "rbp": f(np.asarray(inputs["rel_bias"])[:, PERM]),
        "sinkp": f(np.asarray(inputs["attn_sink"][0])[PERM][None, :]),
        "onehot": onehot, "relvalid": relvalid,
        "w_in": f(inputs["w_in"][0]),
        "bgate": f(np.asarray(inputs["b_gate"][0]).reshape(24, 128).T),
        "dwsc": f(dwsc), "cvec": f(cv),
        "w_co": f(inputs["w_conv_out"][0]), "w_ao": f(w_ao_p),
        "w_mkv": f(inputs["w_mem_kv"][0]), "w_mo": f(inputs["w_mem_out"][0]),
        "w_o": f(inputs["w_o"][0]),
        "ln1": f(np.stack([np.asarray(inputs["ln1_g"][0]), np.asarray(inputs["ln1_b"][0])], 0)),
        "w_f1": f(inputs["w_ffn_in"][0]), "w_f2": f(inputs["w_ffn_out"][0]),
        "ln2": f(np.stack([np.asarray(inputs["ln2_g"][0]), np.asarray(inputs["ln2_b"][0])], 0)),
    }
    x = np.asarray(inputs["x"], np.float32)
    mem = np.asarray(inputs["mem"], np.float32)
    return [dict(shared, x=np.ascontiguousarray(x[b]), mem=np.ascontiguousarray(mem[b])) for b in range(8)]


def kernel(**inputs):
    in_maps = prepare_inputs(inputs)
    nc = build_nc()
    res = run_bass_kernel_spmd(nc, in_maps, core_ids=list(range(8)))
    return np.stack([np.asarray(r["out"], np.float32) for r in res.results], 0)
```
